# Optimizing a Trainium2 kernel written in Bass

```python
import jax
import jax.numpy as jnp
from jax import lax
import numpy as np

D_MODEL = 1024
BATCH = 4
SEQ = 8192
DEPTH = 4

GRID_W = 64
CTX_LEN = 256
N_MIXERS = 3
MIX_CONV = 0
MIX_LRU = 1
MIX_NA = 2
N_CONV_LAYERS = (DEPTH + 2) // 3
N_LRU_LAYERS = (DEPTH + 1) // 3
N_NA_LAYERS = DEPTH // 3
CONV_K = 31
D_RNN = D_MODEL
RNN_BLOCKS = 8
RNN_BLOCK = D_RNN // RNN_BLOCKS
RNN_CONV_K = 4
LRU_C = 8.0
NA_HEADS = 16
NA_HEAD_DIM = D_MODEL // NA_HEADS
NA_KH = 8
NA_KW = 16
D_FF = -(-8 * D_MODEL // (3 * 256)) * 256
EPS = 1e-6

kernel_name = 'hybrid_conv_rglru_natten_dit'


def _rmsnorm(x, g):
    xf = x.astype(jnp.float32)
    y = xf * lax.rsqrt(jnp.mean(xf * xf, axis=-1, keepdims=True) + EPS)
    return (y * g.astype(jnp.float32)).astype(x.dtype)


def _layernorm(x, g, b):
    xf = x.astype(jnp.float32)
    mu = jnp.mean(xf, axis=-1, keepdims=True)
    var = jnp.mean(jnp.square(xf - mu), axis=-1, keepdims=True)
    y = (xf - mu) * lax.rsqrt(var + EPS) * g.astype(jnp.float32) + b.astype(jnp.float32)
    return y.astype(x.dtype)


def _modulation(cond, w_mod, b_mod):
    m = jax.nn.silu(cond) @ w_mod + b_mod
    return jnp.split(m, 6, axis=-1)


def _dwconv(x, w, pad_left, pad_right):
    return lax.conv_general_dilated(
        x, w[:, None, :], window_strides=(1,), padding=[(pad_left, pad_right)],
        dimension_numbers=('NWC', 'WIO', 'NWC'), feature_group_count=x.shape[-1])


def _swiglu(h, w_in, w_out):
    u = h @ w_in
    return (jax.nn.silu(u[..., :D_FF]) * u[..., D_FF:]) @ w_out


def _conformer_conv(h, w_pw1, b_pw1, w_dw, b_dw, ln_g, ln_b, w_pw2, b_pw2):
    u = h @ w_pw1 + b_pw1
    u = u[..., :D_MODEL] * jax.nn.sigmoid(u[..., D_MODEL:])
    u = _dwconv(u, w_dw, CONV_K // 2, CONV_K // 2) + b_dw
    u = jax.nn.silu(_layernorm(u, ln_g, ln_b))
    return u @ w_pw2 + b_pw2


def _lru_coeffs(xr, w_rg, b_rg, w_ig, b_ig, lam):
    shp = xr.shape
    xb = xr.reshape(shp[:-1] + (RNN_BLOCKS, RNN_BLOCK))
    r = jax.nn.sigmoid((jnp.einsum('btni,nij->btnj', xb, w_rg).reshape(shp) + b_rg).astype(jnp.float32))
    gi = jax.nn.sigmoid((jnp.einsum('btni,nij->btnj', xb, w_ig).reshape(shp) + b_ig).astype(jnp.float32))
    log_a = -LRU_C * r * jax.nn.softplus(-lam.astype(jnp.float32))
    a = jnp.exp(log_a)
    b = jnp.sqrt(-jnp.expm1(2.0 * log_a)) * (gi * xr.astype(jnp.float32))
    return a, b


def _linear_scan(a, b, h0, reverse, collect):
    def step(h, ab):
        a_t, b_t = ab
        h = a_t * h + b_t
        return h, (h if collect else None)
    h_last, hs = lax.scan(step, h0, (jnp.swapaxes(a, 0, 1), jnp.swapaxes(b, 0, 1)), reverse=reverse)
    hs = jnp.swapaxes(hs, 0, 1) if collect else None
    return hs, h_last


def _rglru_block(h_lat, h_ctx, ctx_out, w_in, b_in, w_conv, b_conv, w_rg, b_rg, w_ig, b_ig, lam, w_out, b_out):
    pad_l = RNN_CONV_K // 2
    pad_r = RNN_CONV_K - 1 - RNN_CONV_K // 2
    u_l = h_lat @ w_in + b_in
    u_c = h_ctx @ w_in + b_in
    xr_l = _dwconv(u_l[..., D_RNN:], w_conv, pad_l, pad_r) + b_conv
    xr_c = _dwconv(u_c[..., D_RNN:], w_conv, pad_l, pad_r) + b_conv
    h0 = jnp.zeros((h_lat.shape[0], D_RNN), jnp.float32)
    ys_l = []
    ys_c = []
    for d, reverse in ((0, False), (1, True)):
        a_c, b_c = _lru_coeffs(xr_c, w_rg[d], b_rg[d], w_ig[d], b_ig[d], lam[d])
        hs_c, h_c_last = _linear_scan(a_c, b_c, h0, reverse, ctx_out)
        a_l, b_l = _lru_coeffs(xr_l, w_rg[d], b_rg[d], w_ig[d], b_ig[d], lam[d])
        hs_l, _ = _linear_scan(a_l, b_l, h_c_last, reverse, True)
        ys_l.append(hs_l)
        ys_c.append(hs_c)
    y_l = (ys_l[0] + ys_l[1]).astype(h_lat.dtype) * jax.nn.gelu(u_l[..., :D_RNN])
    out_l = y_l @ w_out + b_out
    out_c = None
    if ctx_out:
        y_c = (ys_c[0] + ys_c[1]).astype(h_ctx.dtype) * jax.nn.gelu(u_c[..., :D_RNN])
        out_c = y_c @ w_out + b_out
    return out_l, out_c


def _na_attention(h_lat, h_ctx, ctx_out, w_qkv, b_qkv, rpb, w_o, b_o):
    B, S, _ = h_lat.shape
    rows = S // GRID_W
    kh = min(NA_KH, rows)
    scale = NA_HEAD_DIM ** -0.5

    def qkv(h):
        u = h @ w_qkv + b_qkv
        q, k, v = jnp.split(u, 3, axis=-1)
        shp = h.shape[:2] + (NA_HEADS, NA_HEAD_DIM)
        return q.reshape(shp), k.reshape(shp), v.reshape(shp)

    q, k, v = qkv(h_lat)
    qc, kc, vc = qkv(h_ctx)
    qg = q.reshape(B, rows, GRID_W, NA_HEADS, NA_HEAD_DIM)
    kg = k.reshape(B, rows, GRID_W, NA_HEADS, NA_HEAD_DIM)
    vg = v.reshape(B, rows, GRID_W, NA_HEADS, NA_HEAD_DIM)

    cols = jnp.arange(GRID_W)
    col_start = jnp.clip(cols - NA_KW // 2, 0, GRID_W - NA_KW)
    col_idx = col_start[:, None] + jnp.arange(NA_KW)[None, :]
    col_off = col_idx - cols[:, None] + (NA_KW - 1)
    bias_c = rpb[:, :, col_off].astype(jnp.float32)
    n_loc = kh * NA_KW

    def row_block(r):
        r0 = jnp.clip(r - kh // 2, 0, rows - kh)
        k_strip = lax.dynamic_slice_in_dim(kg, r0, kh, axis=1)
        v_strip = lax.dynamic_slice_in_dim(vg, r0, kh, axis=1)
        k_win = k_strip[:, :, col_idx]
        v_win = v_strip[:, :, col_idx]
        q_r = lax.dynamic_index_in_dim(qg, r, axis=1, keepdims=False)
        row_off = r0 + jnp.arange(kh) - r + (NA_KH - 1)
        bias = jnp.transpose(bias_c[:, row_off], (0, 2, 1, 3))
        s_loc = jnp.einsum('bqhd,brqchd->bhqrc', q_r, k_win).astype(jnp.float32) * scale + bias[None]
        s_ctx = jnp.einsum('bqhd,bkhd->bhqk', q_r, kc).astype(jnp.float32) * scale
        s = jnp.concatenate([s_loc.reshape(B, NA_HEADS, GRID_W, n_loc), s_ctx], axis=-1)
        p = jax.nn.softmax(s, axis=-1).astype(v.dtype)
        p_loc = p[..., :n_loc].reshape(B, NA_HEADS, GRID_W, kh, NA_KW)
        o = jnp.einsum('bhqrc,brqchd->bqhd', p_loc, v_win) + jnp.einsum('bhqk,bkhd->bqhd', p[..., n_loc:], vc)
        return o

    o = lax.map(row_block, jnp.arange(rows))
    o = jnp.transpose(o, (1, 0, 2, 3, 4)).reshape(B, S, D_MODEL)
    out_l = o @ w_o + b_o
    out_c = None
    if ctx_out:
        s_c = jnp.einsum('bqhd,bkhd->bhqk', qc, kc).astype(jnp.float32) * scale
        p_c = jax.nn.softmax(s_c, axis=-1).astype(vc.dtype)
        o_c = jnp.einsum('bhqk,bkhd->bqhd', p_c, vc).reshape(h_ctx.shape[0], h_ctx.shape[1], D_MODEL)
        out_c = o_c @ w_o + b_o
    return out_l, out_c


def setup_inputs(seed: int = 0) -> dict:
    key = jax.random.key(seed)
    ks = iter(jax.random.split(key, 48))

    def nrm(shape, scale):
        return jax.random.normal(next(ks), shape, jnp.float32) * scale

    d = D_MODEL
    x = nrm((BATCH, SEQ, d), 1.0)
    c = nrm((BATCH, d), 1.0)
    ctx = nrm((BATCH, CTX_LEN, d), 1.0)
    c_ctx = nrm((d,), 1.0)
    w_mod = nrm((DEPTH, d, 6 * d), 0.5 * d ** -0.5)
    b_mod = nrm((DEPTH, 6 * d), 0.02)
    norm_g = 1.0 + nrm((DEPTH, 2, d), 0.02)
    w_ffn_in = nrm((DEPTH, d, 2 * D_FF), d ** -0.5)
    w_ffn_out = nrm((DEPTH, D_FF, d), D_FF ** -0.5)
    a_w_pw1 = nrm((N_CONV_LAYERS, d, 2 * d), d ** -0.5)
    a_b_pw1 = nrm((N_CONV_LAYERS, 2 * d), 0.02)
    a_w_dw = nrm((N_CONV_LAYERS, CONV_K, d), CONV_K ** -0.5)
    a_b_dw = nrm((N_CONV_LAYERS, d), 0.02)
    a_ln_g = 1.0 + nrm((N_CONV_LAYERS, d), 0.02)
    a_ln_b = nrm((N_CONV_LAYERS, d), 0.02)
    a_w_pw2 = nrm((N_CONV_LAYERS, d, d), d ** -0.5)
    a_b_pw2 = nrm((N_CONV_LAYERS, d), 0.02)
    b_w_in = nrm((N_LRU_LAYERS, d, 2 * D_RNN), d ** -0.5)
    b_b_in = nrm((N_LRU_LAYERS, 2 * D_RNN), 0.02)
    b_w_conv = nrm((N_LRU_LAYERS, RNN_CONV_K, D_RNN), RNN_CONV_K ** -0.5)
    b_b_conv = nrm((N_LRU_LAYERS, D_RNN), 0.02)
    b_w_rg = nrm((N_LRU_LAYERS, 2, RNN_BLOCKS, RNN_BLOCK, RNN_BLOCK), RNN_BLOCK ** -0.5)
    b_b_rg = nrm((N_LRU_LAYERS, 2, D_RNN), 0.02)
    b_w_ig = nrm((N_LRU_LAYERS, 2, RNN_BLOCKS, RNN_BLOCK, RNN_BLOCK), RNN_BLOCK ** -0.5)
    b_b_ig = nrm((N_LRU_LAYERS, 2, D_RNN), 0.02)
    a0 = jax.random.uniform(next(ks), (N_LRU_LAYERS, 2, D_RNN), jnp.float32, minval=0.9, maxval=0.999)
    s0 = a0 ** (1.0 / LRU_C)
    b_lam = jnp.log(s0) - jnp.log1p(-s0)
    b_w_out = nrm((N_LRU_LAYERS, D_RNN, d), D_RNN ** -0.5)
    b_b_out = nrm((N_LRU_LAYERS, d), 0.02)
    c_w_qkv = nrm((N_NA_LAYERS, d, 3 * d), d ** -0.5)
    c_b_qkv = nrm((N_NA_LAYERS, 3 * d), 0.02)
    c_rpb = nrm((N_NA_LAYERS, NA_HEADS, 2 * NA_KH - 1, 2 * NA_KW - 1), 0.5)
    c_w_o = nrm((N_NA_LAYERS, d, d), d ** -0.5)
    c_b_o = nrm((N_NA_LAYERS, d), 0.02)
    final_g = 1.0 + nrm((d,), 0.02)
    return {'x': x, 'c': c, 'ctx': ctx, 'c_ctx': c_ctx,
            'w_mod': w_mod, 'b_mod': b_mod, 'norm_g': norm_g,
            'w_ffn_in': w_ffn_in, 'w_ffn_out': w_ffn_out,
            'a_w_pw1': a_w_pw1, 'a_b_pw1': a_b_pw1, 'a_w_dw': a_w_dw, 'a_b_dw': a_b_dw,
            'a_ln_g': a_ln_g, 'a_ln_b': a_ln_b, 'a_w_pw2': a_w_pw2, 'a_b_pw2': a_b_pw2,
            'b_w_in': b_w_in, 'b_b_in': b_b_in, 'b_w_conv': b_w_conv, 'b_b_conv': b_b_conv,
            'b_w_rg': b_w_rg, 'b_b_rg': b_b_rg, 'b_w_ig': b_w_ig, 'b_b_ig': b_b_ig,
            'b_lam': b_lam, 'b_w_out': b_w_out, 'b_b_out': b_b_out,
            'c_w_qkv': c_w_qkv, 'c_b_qkv': c_b_qkv, 'c_rpb': c_rpb, 'c_w_o': c_w_o, 'c_b_o': c_b_o,
            'final_g': final_g}


def reference(x, c, ctx, c_ctx, w_mod, b_mod, norm_g, w_ffn_in, w_ffn_out,
              a_w_pw1, a_b_pw1, a_w_dw, a_b_dw, a_ln_g, a_ln_b, a_w_pw2, a_b_pw2,
              b_w_in, b_b_in, b_w_conv, b_b_conv, b_w_rg, b_b_rg, b_w_ig, b_b_ig,
              b_lam, b_w_out, b_b_out,
              c_w_qkv, c_b_qkv, c_rpb, c_w_o, c_b_o, final_g):
    cs = ctx
    for i in range(DEPTH):
        kind = i % N_MIXERS
        j = i // N_MIXERS
        ctx_out = any(k % N_MIXERS != MIX_CONV for k in range(i + 1, DEPTH))
        ctx_in = ctx_out or kind != MIX_CONV
        sh1, sc1, g1, sh2, sc2, g2 = _modulation(c[:, None, :], w_mod[i], b_mod[i])
        h = _rmsnorm(x, norm_g[i, 0]) * (1 + sc1) + sh1
        hc = None
        if ctx_in:
            csh1, csc1, cg1, csh2, csc2, cg2 = _modulation(c_ctx, w_mod[i], b_mod[i])
            hc = _rmsnorm(cs, norm_g[i, 0]) * (1 + csc1) + csh1
        if kind == MIX_CONV:
            conv_p = (a_w_pw1[j], a_b_pw1[j], a_w_dw[j], a_b_dw[j], a_ln_g[j], a_ln_b[j], a_w_pw2[j], a_b_pw2[j])
            y = _conformer_conv(h, *conv_p)
            yc = _conformer_conv(hc, *conv_p) if ctx_out else None
        elif kind == MIX_LRU:
            y, yc = _rglru_block(h, hc, ctx_out, b_w_in[j], b_b_in[j], b_w_conv[j], b_b_conv[j],
                                 b_w_rg[j], b_b_rg[j], b_w_ig[j], b_b_ig[j], b_lam[j], b_w_out[j], b_b_out[j])
        else:
            y, yc = _na_attention(h, hc, ctx_out, c_w_qkv[j], c_b_qkv[j], c_rpb[j], c_w_o[j], c_b_o[j])
        x = x + g1 * y
        x = x + g2 * _swiglu(_rmsnorm(x, norm_g[i, 1]) * (1 + sc2) + sh2, w_ffn_in[i], w_ffn_out[i])
        if ctx_out:
            cs = cs + cg1 * yc
            cs = cs + cg2 * _swiglu(_rmsnorm(cs, norm_g[i, 1]) * (1 + csc2) + csh2, w_ffn_in[i], w_ffn_out[i])
    return _rmsnorm(x, final_g)
```

```python
import contextlib
import numpy as np
import concourse.bass as bass
import concourse.mybir as mybir
from concourse.bass_utils import run_bass_kernel_spmd

F32 = mybir.dt.float32
BF16 = mybir.dt.bfloat16
AF = mybir.ActivationFunctionType
ALU = mybir.AluOpType

D = 1024
NCH = 8
DFF = 2816
NFF = 22
SEQ = 8192
BATCH = 4
CTX = 256
EPS = 1e-6
NCORES = 8

COMPUTE = ("pe", "act", "dve", "pool")
SAME_ENGINE_SYNC = True
NO_SELF_SYNC = ("pe", "act")


class Prog:
    def __init__(self, nc):
        self.nc = nc
        self.ops = {e: [] for e in ("pe", "act", "dve", "pool", "sp")}
        self.cnt = {}
        self.waited = {e: {} for e in self.ops}
        self.keys = {}
        self.stack = contextlib.ExitStack()
        self.sems = {}
        self.dma_streams = []
        self.final = []
        self.semstack = contextlib.ExitStack()
        self.epoch = 0

    def sbuf(self, name, shape, dt):
        self.uid = getattr(self, "uid", 0) + 1
        return self.stack.enter_context(self.nc.sbuf_tensor(f"{name}_u{self.uid}", list(shape), dt))

    def psum(self, name, shape, dt=F32):
        self.uid = getattr(self, "uid", 0) + 1
        return self.stack.enter_context(self.nc.psum_tensor(f"{name}_u{self.uid}", list(shape), dt))

    def _sem(self, key):
        if key not in self.sems:
            self.sems[key] = self.semstack.enter_context(
                self.nc.semaphore("s_" + "".join(ch for ch in str(key) if ch.isalnum() or ch == "_")))
            self.cnt[key] = 0
        return self.sems[key]

    def barrier(self):
        for e in self.ops:
            waits = []
            for k, v in self.cnt.items():
                if v > 0 and self.waited[e].get(k, 0) < v:
                    self.waited[e][k] = v
                    waits.append((k, v))
            if waits:
                self.ops[e].append((waits, None, None))

    @contextlib.contextmanager
    def scope(self):
        outer = self.stack
        self.stack = contextlib.ExitStack()
        try:
            yield
            self.barrier()
            self.epoch += 1
        finally:
            self.stack.close()
            self.stack = outer

    def _deps(self, eng, reads, writes, selfsync=False):
        deps = {}

        def add(d):
            if d is None:
                return
            k, v = d
            if deps.get(k, 0) < v:
                deps[k] = v
        for k in reads:
            st = self.keys.get(k)
            if st is not None:
                add(st[0])
        for k in writes:
            st = self.keys.get(k)
            if st is not None:
                add(st[0])
                for rk, rv in st[1].items():
                    add((rk, rv))
        out = []
        for k, v in deps.items():
            if isinstance(k, tuple) and k[0] == eng and (eng == "pe" or not selfsync) and (
                    eng in NO_SELF_SYNC or not SAME_ENGINE_SYNC):
                continue
            if self.waited[eng].get(k, 0) >= v:
                continue
            self.waited[eng][k] = v
            out.append((k, v))
        return out

    def _commit(self, me, reads, writes):
        for k in reads:
            st = self.keys.setdefault(k, [None, {}])
            if st[1].get(me[0], 0) < me[1]:
                st[1][me[0]] = me[1]
        for k in writes:
            self.keys[k] = [me, {}]

    def op(self, eng, fn, reads=(), writes=(), selfsync=False):
        sk = (eng, self.epoch)
        self._sem(sk)
        waits = self._deps(eng, reads, writes, selfsync)
        self.cnt[sk] += 1
        me = (sk, self.cnt[sk])
        self._commit(me, reads, writes)
        self.ops[eng].append((waits, fn, (sk, 1)))

    def dma(self, q, out, in_, reads=(), writes=(), stream=None, final=False, **kw):
        assert stream is not None
        self._sem(stream)
        waits = self._deps(q, reads, writes)
        prev = self.cnt[stream]
        if prev > 0 and self.waited[q].get(stream, 0) < prev:
            self.waited[q][stream] = prev
            waits.append((stream, prev))
        self.cnt[stream] += 16
        me = (stream, self.cnt[stream])
        self._commit(me, reads, writes)

        def fn(e, out=out, in_=in_, kw=kw):
            return e.dma_start(out=out, in_=in_, **kw)
        self.ops[q].append((waits, fn, (stream, 16)))
        if final:
            self.final.append(me)

    def flush(self):
        nc = self.nc
        for (k, v) in self.final:
            if self.waited["sp"].get(k, 0) < v:
                self.waited["sp"][k] = v
                self.ops["sp"].append(([(k, v)], None, None))
        self.barrier()
        ops = self.ops
        self.ops = {e: [] for e in ops}
        with nc.Block() as block:
            def run(e, name):
                for waits, fn, inc in ops[name]:
                    for (k, v) in waits:
                        e.wait_ge(self.sems[k], v)
                    if fn is None:
                        continue
                    ins = fn(e)
                    ins.then_inc(self.sems[inc[0]], inc[1])

            @block.tensor
            def _(e):
                run(e, "pe")

            @block.scalar
            def _(e):
                run(e, "act")

            @block.vector
            def _(e):
                run(e, "dve")

            @block.gpsimd
            def _(e):
                run(e, "pool")

            @block.sync
            def _(e):
                run(e, "sp")

    def emit(self):
        self.flush()
        self.stack.close()
        self.semstack.close()


def load_weight_bf16(p, wt, wkey, w_dram, kch, n, stream):
    src = w_dram.rearrange("(c p) n -> p c n", p=128)
    step = 2048
    for n0 in range(0, n, step):
        n1 = min(n, n0 + step)
        p.dma("pool", wt[:, :, n0:n1], src[:, :, n0:n1], reads=(), writes=(wkey,),
              stream=stream)


class FFNBufs:
    def __init__(self, p, T=512):
        self.T = T
        self.w_in = p.sbuf("ffn_w_in", [128, NCH, 2 * DFF], BF16)
        self.w_out = p.sbuf("ffn_w_out", [128, NFF, D], BF16)
        self.xt = [p.sbuf(f"ffn_xt{i}", [128, NCH, T], F32) for i in range(2)]
        self.sq = [p.sbuf(f"ffn_sq{i}", [128, T], F32) for i in range(2)]
        self.rstd = p.sbuf("ffn_rstd", [128, T], F32)
        self.tmp = self.sq
        self.h = p.sbuf("ffn_h", [128, NCH, T], BF16)
        self.a = p.sbuf("ffn_a", [128, NFF, T], BF16)
        self.sg = self.sq
        self.ps_ss = p.psum("ffn_ps_ss", [128, T])
        self.ps_g = [p.psum(f"ffn_ps_g{i}", [128, T]) for i in range(2)]
        self.ps_u = [p.psum(f"ffn_ps_u{i}", [128, T]) for i in range(2)]
        self.ps_o = [p.psum(f"ffn_ps_o{i}", [128, T]) for i in range(2)]


def rmsnorm_mod(p, pre, xt, xkey, T, sq, rstd, tmp, ps_ss, ones, h, hkey, gs, sh, gskey):
    for c in range(NCH):
        s = sq[c % 2]
        p.op("act", lambda e, s=s, c=c: e.activation(out=s[:, :T], in_=xt[:, c, :T], func=AF.Square),
             reads=(xkey,), writes=(pre + f"sq{c % 2}",))
        p.op("pe", lambda e, s=s, c=c: e.matmul(ps_ss[:, :T], ones[:, :], s[:, :T],
                                                start=(c == 0), stop=(c == NCH - 1)),
             reads=(pre + f"sq{c % 2}", "ones"), writes=(pre + "ps_ss",))
    p.op("act", lambda e: e.activation(out=rstd[:, :T], in_=ps_ss[:, :T], func=AF.Sqrt,
                                       bias=p.eps_ap, scale=1.0 / D),
         reads=(pre + "ps_ss", "eps"), writes=(pre + "rstd",))
    p.op("dve", lambda e: e.reciprocal(out=rstd[:, :T], in_=rstd[:, :T]),
         reads=(pre + "rstd",), writes=(pre + "rstd",))
    for c in range(NCH):
        t = tmp[c % 2]
        p.op("dve", lambda e, t=t, c=c: e.scalar_tensor_tensor(
            out=t[:, :T], in0=xt[:, c, :T], scalar=gs[:, c:c + 1], in1=rstd[:, :T],
            op0=ALU.mult, op1=ALU.mult),
            reads=(xkey, "vecs", "vecs_der", pre + "rstd"), writes=(pre + f"sq{c % 2}",))
        p.op("act", lambda e, t=t, c=c: e.activation(out=h[:, c, :T], in_=t[:, :T], func=AF.Identity,
                                                     bias=sh[:, c:c + 1], scale=1.0),
             reads=(pre + f"sq{c % 2}", "vecs", "vecs_der"), writes=(hkey,))


def ffn_tile(p, fb, xt, xkey, T, gs, sh, g2, modkey):
    rmsnorm_mod(p, "ffn_", xt, xkey, T, fb.sq, fb.rstd, fb.tmp, fb.ps_ss, p.ones, fb.h, "ffn_h",
                gs, sh, modkey)
    for j in range(NFF):
        pg, pu, sg = fb.ps_g[j % 2], fb.ps_u[j % 2], fb.sg[j % 2]
        for k in range(NCH):
            p.op("pe", lambda e, pg=pg, j=j, k=k: e.matmul(
                pg[:, :T], fb.w_in[:, k, j * 128:(j + 1) * 128], fb.h[:, k, :T],
                start=(k == 0), stop=(k == NCH - 1)),
                reads=("ffn_w_in", "ffn_h"), writes=(f"ffn_ps_g{j % 2}",))
        for k in range(NCH):
            p.op("pe", lambda e, pu=pu, j=j, k=k: e.matmul(
                pu[:, :T], fb.w_in[:, k, DFF + j * 128:DFF + (j + 1) * 128], fb.h[:, k, :T],
                start=(k == 0), stop=(k == NCH - 1)),
                reads=("ffn_w_in", "ffn_h"), writes=(f"ffn_ps_u{j % 2}",))
        p.op("act", lambda e, pg=pg, sg=sg: e.activation(out=sg[:, :T], in_=pg[:, :T], func=AF.Silu),
             reads=(f"ffn_ps_g{j % 2}",), writes=(f"ffn_sq{j % 2}",))
        p.op("dve", lambda e, pu=pu, sg=sg, j=j: e.tensor_tensor(
            out=fb.a[:, j, :T], in0=sg[:, :T], in1=pu[:, :T], op=ALU.mult),
            reads=(f"ffn_sq{j % 2}", f"ffn_ps_u{j % 2}"), writes=(f"ffn_a{j}",))
    for n in range(NCH):
        po = fb.ps_o[n % 2]
        for j in range(NFF):
            p.op("pe", lambda e, po=po, n=n, j=j: e.matmul(
                po[:, :T], fb.w_out[:, j, n * 128:(n + 1) * 128], fb.a[:, j, :T],
                start=(j == 0), stop=(j == NFF - 1)),
                reads=("ffn_w_out", f"ffn_a{j}"), writes=(f"ffn_ps_o{n % 2}",))
        p.op("dve", lambda e, po=po, n=n: e.scalar_tensor_tensor(
            out=xt[:, n, :T], in0=po[:, :T], scalar=g2[:, n:n + 1], in1=xt[:, n, :T],
            op0=ALU.mult, op1=ALU.add),
            reads=(f"ffn_ps_o{n % 2}", xkey, "vecs", "vecs_der"), writes=(xkey,))


def setup_consts(p):
    p.ones = p.sbuf("ones", [128, 128], F32)
    p.eps_t = p.sbuf("eps", [128, 1], F32)
    p.eps_ap = p.eps_t[:, 0:1]
    p.op("dve", lambda e: e.memset(p.ones[:, :], 1.0), writes=("ones",))
    p.op("dve", lambda e: e.memset(p.eps_t[:, :], EPS), writes=("eps",))


def fm(v):
    return np.ascontiguousarray(np.asarray(v, np.float32).reshape(-1, 128).T)


class Launch:
    def __init__(self):
        self.nc = bass.Bass("TRN2", target_bir_lowering=False)
        self.p = Prog(self.nc)
        self.vec_spec = []
        self.vec_off = {}
        self.nv = 0

    def inp(self, name, shape, dt=F32):
        return self.nc.dram_tensor(name, list(shape), dt, kind="ExternalInput").ap()

    def out(self, name, shape, dt=F32):
        return self.nc.dram_tensor(name, list(shape), dt, kind="ExternalOutput").ap()

    def declare_vecs(self, spec):
        for name, n in spec:
            self.vec_off[name] = (self.nv, n)
            self.nv += n
        self.vecs_dram = self.inp("vecs", [128, self.nv])

    def preamble(self, fused=False):
        p = self.p
        setup_consts(p)
        p.one_t = p.sbuf("one_c", [128, 1], F32)
        p.zero_t = p.sbuf("zero_c", [128, 1], F32)
        p.op("dve", lambda e: e.memset(p.one_t[:, :], 1.0), writes=("onec",))
        p.op("dve", lambda e: e.memset(p.zero_t[:, :], 0.0), writes=("zeroc",))
        self.vt = p.sbuf("vecs_t", [128, self.nv], F32)
        p.dma("sp", self.vt[:, :], self.vecs_dram[:, :], writes=("vecs",), stream="ld_vecs")
        self.mt = p.sbuf("modv_t", [128, 4 * 2 * 48], F32)
        if not fused:
            self.modv_dram = self.inp("modv", [128, 4 * 2 * 48])
            p.dma("sp", self.mt[:, :], self.modv_dram[:, :], writes=("vecs",), stream="ld_modv")
        self.der = p.sbuf("der_t", [128, 16 * 8], F32)
        self.nder = 0

    def V(self, name, i0=0, n=None):
        off, w = self.vec_off[name]
        n = w - i0 if n is None else n
        return self.vt[:, off + i0: off + i0 + n]

    def mod(self, layer, who, idx):
        b = (layer * 2 + who) * 48 + idx * 8
        return self.mt[:, b:b + 8]

    def gscale(self, layer, who, which):
        p = self.p
        o = self.der[:, self.nder * 8:(self.nder + 1) * 8]
        self.nder += 1
        sc = self.mod(layer, who, 1 if which == 0 else 4)
        ng = self.V(f"ng{layer}_{which}")
        p.op("dve", lambda e: e.scalar_tensor_tensor(out=o, in0=sc, scalar=1.0, in1=ng,
                                                     op0=ALU.add, op1=ALU.mult),
             reads=("vecs",), writes=("vecs_der",))
        return o

    def derived(self, fn):
        o = self.der[:, self.nder * 8:(self.nder + 1) * 8]
        self.nder += 1
        self.p.op("dve", lambda e: fn(e, o), reads=("vecs",), writes=("vecs_der",))
        return o


MODKEYS = ("vecs", "vecs_der")


def run_launch(L, in_maps):
    L.p.emit()
    res = run_bass_kernel_spmd(L.nc, in_maps, core_ids=list(range(NCORES)))
    return res.results


def build_mod():
    L = Launch()
    p = L.p
    cv = L.inp("cv", [128, 16])
    bm = L.inp("bm", [128, 4 * 48])
    w_mod = L.inp("w_mod", [4, D, 6 * D])
    modv = L.out("modv", [128, 4 * 2 * 48])
    cvt = p.sbuf("cvt", [128, 8, 2], F32)
    bmt = p.sbuf("bmt", [128, 4 * 48], F32)
    mo = p.sbuf("mo", [128, 4, 2, 48], F32)
    wb = [p.sbuf(f"wb{i}", [128, 8, 768], F32) for i in range(2)]
    ps = [p.psum(f"psm{i}", [128, 48, 2]) for i in range(2)]
    p.dma("sp", cvt[:, :, :], cv.rearrange("p (k w) -> p k w", w=2), writes=("cv",), stream="ld_cv")
    p.dma("sp", bmt[:, :], bm[:, :], writes=("bm",), stream="ld_bm")
    p.op("act", lambda e: e.activation(out=cvt[:, :, :], in_=cvt[:, :, :], func=AF.Silu),
         reads=("cv",), writes=("cv",))
    n = 0
    for i in range(4):
        wv = w_mod[i].rearrange("(c p) n -> p c n", p=128)
        for pc in range(8):
            b = wb[n % 2]
            p.dma("sp", b[:, :, :], wv[:, :, pc * 768:(pc + 1) * 768], writes=(f"wb{n % 2}",),
                  stream=f"ld_wb{n % 2}")
            for jj in range(6):
                j = pc * 6 + jj
                for kc in range(8):
                    p.op("pe", lambda e, b=b, jj=jj, j=j, kc=kc, i=i: e.matmul(
                        ps[i % 2][:, j, :], b[:, kc, jj * 128:(jj + 1) * 128], cvt[:, kc, :],
                        start=(kc == 0), stop=(kc == 7)),
                        reads=(f"wb{n % 2}", "cv"), writes=(f"psm{i % 2}",))
            n += 1
        for who in range(2):
            p.op("dve", lambda e, i=i, who=who: e.tensor_tensor(
                out=mo[:, i, who, :], in0=ps[i % 2][:, :, who], in1=bmt[:, i * 48:(i + 1) * 48],
                op=ALU.add), reads=(f"psm{i % 2}", "bm"), writes=("mo",))
    p.dma("sp", modv[:, :], mo[:, :, :, :].rearrange("p a b c -> p (a b c)"), reads=("mo",),
          stream="st_mo", final=True)
    return L


def ffn_stage(L, layer, w_in_d, w_out_d, tiles):
    p = L.p
    with p.scope():
        fb = FFNBufs(p)
        load_weight_bf16(p, fb.w_in, "ffn_w_in", w_in_d, NCH, 2 * DFF, "ld_w_in")
        load_weight_bf16(p, fb.w_out, "ffn_w_out", w_out_d, NFF, D, "ld_w_out")
        mods = {}
        for who in sorted(set(t[5] for t in tiles)):
            mods[who] = (L.gscale(layer, who, 1), L.mod(layer, who, 3), L.mod(layer, who, 5))
        def load(n):
            (src, s0, dst, d0, T, who) = tiles[n]
            sv = src.rearrange("(c p) t -> p c t", p=128)
            p.dma("sp", fb.xt[n % 2][:, :, :T], sv[:, :, s0:s0 + T], writes=(f"ffn_xt{n % 2}",), stream=f"ld_x{n % 2}")
        load(0)
        for n, (src, s0, dst, d0, T, who) in enumerate(tiles):
            xt = fb.xt[n % 2]
            xkey = f"ffn_xt{n % 2}"
            dv = dst.rearrange("(c p) t -> p c t", p=128)
            if n + 1 < len(tiles):
                load(n + 1)
            gs, sh, g2 = mods[who]
            ffn_tile(p, fb, xt, xkey, T, gs, sh, g2, "vecs_der")
            p.dma("sp", dv[:, :, d0:d0 + T], xt[:, :, :T], reads=(xkey,), stream=f"st_x{n % 2}",
                  final=True)


def tok_tiles(total, step):
    out = []
    t = 0
    while t < total:
        out.append((t, min(step, total - t)))
        t += step
    return out


CONV_K = 31
CONV_TO = 512 - (CONV_K - 1)


def conv_stage(L, layer, j, w1_d, w2_d, ident_d, tiles):
    p = L.p
    pre = f"a{j}_"
    with p.scope():
        w1 = p.sbuf("cv_w1", [128, NCH, 2 * D], BF16)
        w2 = p.sbuf("cv_w2", [128, NCH, D], BF16)
        load_weight_bf16(p, w1, "cv_w1", w1_d, NCH, 2 * D, "ld_cw1")
        load_weight_bf16(p, w2, "cv_w2", w2_d, NCH, D, "ld_cw2")
        idf = p.sbuf("cv_idf", [128, 128], F32)
        p.dma("sp", idf[:, :], ident_d[:, :], writes=("cv_idf",), stream="ld_idf")
        diag = p.sbuf("cv_diag", [128, NCH, CONV_K, 128], BF16)
        wdw = L.V(pre + "wdw")
        for c in range(NCH):
            for k in range(CONV_K):
                p.op("pool", lambda e, c=c, k=k: e.tensor_scalar(
                    out=diag[:, c, k, :], in0=idf[:, :], scalar1=wdw[:, k * 8 + c:k * 8 + c + 1],
                    scalar2=None, op0=ALU.mult),
                    reads=("cv_idf", "vecs"), writes=("cv_diag",))
        xts = [p.sbuf(f"cv_xt{i}", [128, NCH, 512], F32) for i in range(2)]
        h = p.sbuf("cv_h", [128, NCH, 512], BF16)
        u = p.sbuf("cv_u", [128, NCH, 512], BF16)
        v = p.sbuf("cv_v", [128, NCH, 512], F32)
        z = p.sbuf("cv_z", [128, NCH, 512], BF16)
        sq = [p.sbuf(f"cv_sq{i}", [128, 512], F32) for i in range(2)]
        rstd = p.sbuf("cv_rstd", [128, 512], F32)
        mean = p.sbuf("cv_mean", [128, 512], F32)
        lrs = p.sbuf("cv_lrs", [128, 512], F32)
        ps_ss = p.psum("cv_ps_ss", [128, 512])
        ps_a2 = [p.psum(f"cv_ps_a{i}", [128, 512]) for i in range(2)]
        ps_b2 = [p.psum(f"cv_ps_b{i}", [128, 512]) for i in range(2)]
        ps_c2 = [p.psum(f"cv_ps_c{i}", [128, 512]) for i in range(2)]
        ps_m = ps_ss
        ps_q = p.psum("cv_ps_q", [128, 512])
        b1, bdw, lng, lnb, b2 = (L.V(pre + n) for n in ("b1", "bdw", "lng", "lnb", "b2"))
        mods = {}
        for who in sorted(set(t[5] for t in tiles)):
            mods[who] = (L.gscale(layer, who, 0), L.mod(layer, who, 0), L.mod(layer, who, 2))
        RD = ("vecs", "vecs_der")
        def load(n):
            (src, s0, dst, d0, To, who, mL, mR) = tiles[n]
            sv = src.rearrange("(c p) t -> p c t", p=128)
            p.dma("sp", xts[n % 2][:, :, :To + CONV_K - 1], sv[:, :, s0:s0 + To + CONV_K - 1], writes=(f"cv_xt{n % 2}",),
                  stream=f"ld_cvx{n % 2}")

        def body(n, src, s0, dst, d0, To, who, mL, mR):
            Tu = To + CONV_K - 1
            xt = xts[n % 2]
            XK = f"cv_xt{n % 2}"
            if n + 1 < len(tiles):
                load(n + 1)
            gs, sh, g1 = mods[who]
            sv = src.rearrange("(c p) t -> p c t", p=128)
            dv = dst.rearrange("(c p) t -> p c t", p=128)
            rmsnorm_mod(p, "cv_", xt, XK, Tu, sq, rstd, sq, ps_ss, p.ones, h, "cv_h", gs, sh, None)
            for c in range(NCH):
                ps_a, ps_b = ps_a2[c % 2], ps_b2[c % 2]
                ka, kb_ = f"cv_ps_a{c % 2}", f"cv_ps_b{c % 2}"
                for k in range(NCH):
                    p.op("pe", lambda e, c=c, k=k, ps_a=ps_a: e.matmul(
                        ps_a[:, :Tu], w1[:, k, c * 128:(c + 1) * 128], h[:, k, :Tu],
                        start=(k == 0), stop=(k == NCH - 1)),
                        reads=("cv_w1", "cv_h"), writes=(ka,))
                for k in range(NCH):
                    p.op("pe", lambda e, c=c, k=k, ps_b=ps_b: e.matmul(
                        ps_b[:, :Tu], w1[:, k, D + c * 128:D + (c + 1) * 128], h[:, k, :Tu],
                        start=(k == 0), stop=(k == NCH - 1)),
                        reads=("cv_w1", "cv_h"), writes=(kb_,))
                s = sq[c % 2]
                p.op("act", lambda e, s=s, c=c, ps_b=ps_b: e.activation(
                    out=s[:, :Tu], in_=ps_b[:, :Tu], func=AF.Sigmoid, bias=b1[:, 8 + c:9 + c], scale=1.0),
                    reads=(kb_,) + RD, writes=(f"cv_sq{c % 2}",))
                p.op("dve", lambda e, s=s, c=c, ps_a=ps_a: e.scalar_tensor_tensor(
                    out=u[:, c, :Tu], in0=ps_a[:, :Tu], scalar=b1[:, c:c + 1], in1=s[:, :Tu],
                    op0=ALU.add, op1=ALU.mult),
                    reads=(ka, f"cv_sq{c % 2}") + RD, writes=(f"cv_u{c}",))
                if mL is not None:
                    p.op("dve", lambda e, c=c, mL=mL: e.tensor_scalar(
                        out=u[:, c, 0:15], in0=u[:, c, 0:15], scalar1=mL, scalar2=None, op0=ALU.mult),
                        reads=(f"cv_u{c}",) + RD, writes=(f"cv_u{c}",))
                if mR is not None:
                    p.op("dve", lambda e, c=c, mR=mR: e.tensor_scalar(
                        out=u[:, c, Tu - 15:Tu], in0=u[:, c, Tu - 15:Tu], scalar1=mR, scalar2=None,
                        op0=ALU.mult),
                        reads=(f"cv_u{c}",) + RD, writes=(f"cv_u{c}",))
            for c in range(NCH):
                ps_c = ps_c2[c % 2]
                kc_ = f"cv_ps_c{c % 2}"
                for k in range(CONV_K):
                    p.op("pe", lambda e, c=c, k=k, ps_c=ps_c: e.matmul(
                        ps_c[:, :To], diag[:, c, k, :], u[:, c, k:k + To],
                        start=(k == 0), stop=(k == CONV_K - 1)),
                        reads=("cv_diag", f"cv_u{c}"), writes=(kc_,))
                s = sq[c % 2]
                p.op("act", lambda e, c=c, ps_c=ps_c: e.activation(
                    out=v[:, c, :To], in_=ps_c[:, :To], func=AF.Identity, bias=bdw[:, c:c + 1], scale=1.0),
                    reads=(kc_,) + RD, writes=(f"cv_v{c}",))
                p.op("act", lambda e, c=c, s=s, ps_c=ps_c: e.activation(
                    out=s[:, :To], in_=ps_c[:, :To], func=AF.Square, bias=bdw[:, c:c + 1], scale=1.0),
                    reads=(kc_,) + RD, writes=(f"cv_sq{c % 2}",))
                p.op("pe", lambda e, c=c: e.matmul(ps_m[:, :To], p.ones[:, :], v[:, c, :To],
                                                   start=(c == 0), stop=(c == NCH - 1)),
                     reads=(f"cv_v{c}", "ones"), writes=("cv_ps_ss",))
                p.op("pe", lambda e, c=c, s=s: e.matmul(ps_q[:, :To], p.ones[:, :], s[:, :To],
                                                        start=(c == 0), stop=(c == NCH - 1)),
                     reads=(f"cv_sq{c % 2}", "ones"), writes=("cv_ps_q",))
            p.op("act", lambda e: e.activation(out=mean[:, :To], in_=ps_m[:, :To], func=AF.Identity,
                                               bias=p.zero_t[:, 0:1], scale=1.0 / D),
                 reads=("cv_ps_ss", "zeroc"), writes=("cv_mean",))
            p.op("dve", lambda e: e.tensor_tensor(out=lrs[:, :To], in0=mean[:, :To], in1=mean[:, :To],
                                                  op=ALU.mult),
                 reads=("cv_mean",), writes=("cv_lrs",))
            p.op("dve", lambda e: e.scalar_tensor_tensor(
                out=lrs[:, :To], in0=ps_q[:, :To], scalar=1.0 / D, in1=lrs[:, :To],
                op0=ALU.mult, op1=ALU.subtract),
                reads=("cv_ps_q", "cv_lrs"), writes=("cv_lrs",))
            p.op("act", lambda e: e.activation(out=lrs[:, :To], in_=lrs[:, :To], func=AF.Sqrt,
                                               bias=p.eps_ap, scale=1.0),
                 reads=("cv_lrs", "eps"), writes=("cv_lrs",))
            p.op("dve", lambda e: e.reciprocal(out=lrs[:, :To], in_=lrs[:, :To]),
                 reads=("cv_lrs",), writes=("cv_lrs",))
            for c in range(NCH):
                s = sq[c % 2]
                p.op("dve", lambda e, c=c, s=s: e.tensor_tensor(
                    out=s[:, :To], in0=v[:, c, :To], in1=mean[:, :To], op=ALU.subtract),
                    reads=(f"cv_v{c}", "cv_mean"), writes=(f"cv_sq{c % 2}",))
                p.op("dve", lambda e, c=c, s=s: e.scalar_tensor_tensor(
                    out=s[:, :To], in0=s[:, :To], scalar=lng[:, c:c + 1], in1=lrs[:, :To],
                    op0=ALU.mult, op1=ALU.mult),
                    reads=(f"cv_sq{c % 2}", "cv_lrs") + RD, writes=(f"cv_sq{c % 2}",))
                p.op("act", lambda e, c=c, s=s: e.activation(
                    out=z[:, c, :To], in_=s[:, :To], func=AF.Silu, bias=lnb[:, c:c + 1], scale=1.0),
                    reads=(f"cv_sq{c % 2}",) + RD, writes=(f"cv_z{c}",))
            for n in range(NCH):
                ps_o = ps_a2[n % 2]
                ko = f"cv_ps_a{n % 2}"
                for k in range(NCH):
                    p.op("pe", lambda e, n=n, k=k, ps_o=ps_o: e.matmul(
                        ps_o[:, :To], w2[:, k, n * 128:(n + 1) * 128], z[:, k, :To],
                        start=(k == 0), stop=(k == NCH - 1)),
                        reads=("cv_w2", f"cv_z{k}"), writes=(ko,))
                s = sq[n % 2]
                p.op("act", lambda e, n=n, s=s, ps_o=ps_o: e.activation(
                    out=s[:, :To], in_=ps_o[:, :To], func=AF.Identity, bias=b2[:, n:n + 1], scale=1.0),
                    reads=(ko,) + RD, writes=(f"cv_sq{n % 2}",))
                p.op("dve", lambda e, n=n, s=s: e.scalar_tensor_tensor(
                    out=v[:, n, :To], in0=s[:, :To], scalar=g1[:, n:n + 1], in1=xt[:, n, 15:15 + To],
                    op0=ALU.mult, op1=ALU.add),
                    reads=(f"cv_sq{n % 2}", XK) + RD, writes=(f"cv_v{n}",))
            p.dma("sp", dv[:, :, d0:d0 + To], v[:, :, :To], reads=tuple(f"cv_v{c}" for c in range(NCH)),
                  stream="st_cv", final=True)
        load(0)
        for n_, t_ in enumerate(tiles):
            body(n_, *t_)


LRU_TO = 508


def lruA_stage(L, layer, w_in_d, w_rg_d, w_ig_d, tiles, outs, ea_d, ntt):
    p = L.p
    RD = ("vecs", "vecs_der")
    with p.scope():
        win = p.sbuf("lr_win", [128, NCH, 2 * D], BF16)
        load_weight_bf16(p, win, "lr_win", w_in_d, NCH, 2 * D, "ld_lwin")
        wrg = p.sbuf("lr_wrg", [128, 16, 128], BF16)
        wig = p.sbuf("lr_wig", [128, 16, 128], BF16)
        p.dma("pool", wrg[:, :, :], w_rg_d.rearrange("d n i j -> i (d n) j"), writes=("lr_wrg",), stream="ld_wrg")
        p.dma("pool", wig[:, :, :], w_ig_d.rearrange("d n i j -> i (d n) j"), writes=("lr_wig",), stream="ld_wig")
        zeros = p.sbuf("lr_zeros", [128, 512], F32)
        p.op("pool", lambda e: e.memset(zeros[:, :], 0.0), writes=("lr_zeros",))
        cn = p.sbuf("lr_cn", [128, 48], F32)
        lam = L.V("b_lam")
        p.op("act", lambda e: e.activation(out=cn[:, 32:48], in_=lam, func=AF.Exp, bias=p.zero_t[:, 0:1], scale=-1.0),
             reads=RD + ("zeroc",), writes=("lr_cn",))
        p.op("act", lambda e: e.activation(out=cn[:, 32:48], in_=cn[:, 32:48], func=AF.Ln, bias=p.one_t[:, 0:1], scale=1.0),
             reads=("lr_cn", "onec"), writes=("lr_cn",), selfsync=True)
        p.op("dve", lambda e: e.tensor_scalar(out=cn[:, 0:16], in0=cn[:, 32:48], scalar1=-8.0, scalar2=None, op0=ALU.mult),
             reads=("lr_cn",), writes=("lr_cn",))
        p.op("dve", lambda e: e.tensor_scalar(out=cn[:, 16:32], in0=cn[:, 32:48], scalar1=-16.0, scalar2=None, op0=ALU.mult),
             reads=("lr_cn",), writes=("lr_cn",))
        xt = p.sbuf("lr_xt", [128, NCH, 512], F32)
        tas = [xt[:, c_, :] for c_ in range(NCH)]
        h = p.sbuf("lr_h", [128, NCH, 512], BF16)
        gl = p.sbuf("lr_gl", [128, NCH, 512], F32)
        xr = p.sbuf("lr_xr", [128, NCH, 512], F32)
        xrb = p.sbuf("lr_xrb", [128, NCH, 512], BF16)
        st = p.sbuf("lr_s", [128, NCH, 512], F32)
        pf = p.sbuf("lr_pf", [128, NCH, 512], F32)
        pb = p.sbuf("lr_pb", [128, NCH, 512], F32)
        sq = [p.sbuf(f"lr_sq{i}", [128, 512], F32) for i in range(2)]
        rstd = p.sbuf("lr_rstd", [128, 512], F32)
        trs = [p.sbuf(f"lr_tr{i}", [128, 512], F32) for i in range(4)]
        tgs = [p.sbuf(f"lr_tg{i}", [128, 512], F32) for i in range(4)]
        tbs = [p.sbuf(f"lr_tb{i}", [128, 512], F32) for i in range(8)]
        ths = [p.sbuf(f"lr_th{i}", [128, 512], F32) for i in range(2)]
        ea = p.sbuf("lr_ea", [128, 4, 8, ntt], F32)
        ps_ss = p.psum("lr_ps_ss", [128, 512])
        ps_a = p.psum("lr_ps_a", [128, 512])
        ps_b = p.psum("lr_ps_b", [128, 512])
        ps_rs = [p.psum(f"lr_ps_r{i}", [128, 512]) for i in range(2)]
        ps_is = [p.psum(f"lr_ps_i{i}", [128, 512]) for i in range(2)]
        b_in, wc, bcv, brg, big = (L.V(n) for n in ("b_bin", "b_wc", "b_bcv", "b_brg", "b_big"))
        mods = {}
        for who in sorted(set(t[3] for t in tiles)):
            mods[who] = (L.gscale(layer, who, 0), L.mod(layer, who, 0))
        def body(src, s0, To, who, mL, mR, d0, kidx):
            Tu = To + 4
            gs, sh = mods[who]
            sv = src.rearrange("(c p) t -> p c t", p=128)
            p.dma("sp", xt[:, :, :Tu], sv[:, :, s0:s0 + Tu],
                  writes=("lr_xt",) + tuple(f"lr_ta{c_}" for c_ in range(NCH)), stream="ld_lrx")
            rmsnorm_mod(p, "lr_", xt, "lr_xt", Tu, sq, rstd, sq, ps_ss, p.ones, h, "lr_h", gs, sh, None)
            for c in range(NCH):
                pq = (ps_a, ps_b)[c % 2]
                pk = ("lr_ps_a", "lr_ps_b")[c % 2]
                for k in range(NCH):
                    p.op("pe", lambda e, c=c, k=k, pq=pq: e.matmul(
                        pq[:, :To], win[:, k, c * 128:(c + 1) * 128], h[:, k, 2:2 + To],
                        start=(k == 0), stop=(k == NCH - 1)),
                        reads=("lr_win", "lr_h"), writes=(pk,))
                p.op("act", lambda e, c=c, pq=pq: e.activation(
                    out=gl[:, c, :To], in_=pq[:, :To], func=AF.Gelu, bias=b_in[:, c:c + 1], scale=1.0),
                    reads=(pk,) + RD, writes=("lr_gl",))
            for c in range(NCH):
                pq = (ps_a, ps_b)[c % 2]
                pk = ("lr_ps_a", "lr_ps_b")[c % 2]
                for k in range(NCH):
                    p.op("pe", lambda e, c=c, k=k, pq=pq: e.matmul(
                        pq[:, :Tu], win[:, k, D + c * 128:D + (c + 1) * 128], h[:, k, :Tu],
                        start=(k == 0), stop=(k == NCH - 1)),
                        reads=("lr_win", "lr_h"), writes=(pk,))
                s = sq[c % 2]
                sk = f"lr_sq{c % 2}"
                p.op("act", lambda e, c=c, s=s, pq=pq: e.activation(
                    out=s[:, :Tu], in_=pq[:, :Tu], func=AF.Identity, bias=b_in[:, 8 + c:9 + c], scale=1.0),
                    reads=(pk,) + RD, writes=(sk,))
                if mL is not None:
                    p.op("dve", lambda e, s=s, mL=mL: e.tensor_scalar(
                        out=s[:, 0:2], in0=s[:, 0:2], scalar1=mL, scalar2=None, op0=ALU.mult),
                        reads=(sk,) + RD, writes=(sk,))
                if mR is not None:
                    p.op("dve", lambda e, s=s, mR=mR: e.tensor_scalar(
                        out=s[:, Tu - 2:Tu], in0=s[:, Tu - 2:Tu], scalar1=mR, scalar2=None, op0=ALU.mult),
                        reads=(sk,) + RD, writes=(sk,))
                p.op("dve", lambda e, c=c, s=s: e.tensor_scalar(
                    out=xr[:, c, :To], in0=s[:, 0:To], scalar1=wc[:, c:c + 1], scalar2=bcv[:, c:c + 1],
                    op0=ALU.mult, op1=ALU.add),
                    reads=(sk,) + RD, writes=(f"lr_xr{c}",))
                for k in range(1, 5):
                    p.op("dve", lambda e, c=c, s=s, k=k: e.scalar_tensor_tensor(
                        out=xr[:, c, :To], in0=s[:, k:k + To], scalar=wc[:, k * 8 + c:k * 8 + c + 1],
                        in1=xr[:, c, :To], op0=ALU.mult, op1=ALU.add),
                        reads=(sk, f"lr_xr{c}") + RD, writes=(f"lr_xr{c}",))
                p.op("act", lambda e, c=c: e.activation(
                    out=xrb[:, c, :To], in_=xr[:, c, :To], func=AF.Identity, bias=p.zero_t[:, 0:1], scale=1.0),
                    reads=(f"lr_xr{c}", "zeroc"), writes=(f"lr_xrb{c}",))
            for d in range(2):
                for g in range(2):
                    cs_ = list(range(4 * g, 4 * g + 4))
                    for c in cs_:
                        i = d * 8 + c
                        par = c % 2
                        ps_r, ps_i = ps_rs[par], ps_is[par]
                        KPR, KPI = f"lr_ps_r{par}", f"lr_ps_i{par}"
                        tr, tg = trs[c % 4], tgs[c % 4]
                        p.op("pe", lambda e, c=c, i=i, ps_r=ps_r: e.matmul(ps_r[:, :To], wrg[:, i, :], xrb[:, c, :To],
                                                                           start=True, stop=True),
                             reads=("lr_wrg", f"lr_xrb{c}"), writes=(KPR,))
                        p.op("pe", lambda e, c=c, i=i, ps_i=ps_i: e.matmul(ps_i[:, :To], wig[:, i, :], xrb[:, c, :To],
                                                                           start=True, stop=True),
                             reads=("lr_wig", f"lr_xrb{c}"), writes=(KPI,))
                        p.op("act", lambda e, i=i, tr=tr, ps_r=ps_r: e.activation(
                            out=tr[:, :To], in_=ps_r[:, :To], func=AF.Sigmoid, bias=brg[:, i:i + 1], scale=1.0),
                            reads=(KPR,) + RD, writes=(f"lr_tr{c % 4}",))
                        p.op("act", lambda e, i=i, tg=tg, ps_i=ps_i: e.activation(
                            out=tg[:, :To], in_=ps_i[:, :To], func=AF.Sigmoid, bias=big[:, i:i + 1], scale=1.0),
                            reads=(KPI,) + RD, writes=(f"lr_tg{c % 4}",))
                    for c in cs_:
                        i = d * 8 + c
                        tr, ta, tb = trs[c % 4], tas[c], tbs[c]
                        p.op("act", lambda e, i=i, tr=tr, ta=ta: e.activation(
                            out=ta[:, :To], in_=tr[:, :To], func=AF.Exp, bias=p.zero_t[:, 0:1], scale=cn[:, i:i + 1]),
                            reads=(f"lr_tr{c % 4}", "lr_cn", "zeroc"), writes=(f"lr_ta{c}",))
                        p.op("act", lambda e, i=i, tr=tr, tb=tb: e.activation(
                            out=tb[:, :To], in_=tr[:, :To], func=AF.Exp, bias=p.zero_t[:, 0:1], scale=cn[:, 16 + i:17 + i]),
                            reads=(f"lr_tr{c % 4}", "lr_cn", "zeroc"), writes=(f"lr_tb{c}",))
                    for c in cs_:
                        tb = tbs[c]
                        p.op("act", lambda e, tb=tb: e.activation(
                            out=tb[:, :To], in_=tb[:, :To], func=AF.Sqrt, bias=p.one_t[:, 0:1], scale=-1.0),
                            reads=(f"lr_tb{c}", "onec"), writes=(f"lr_tb{c}",))
                    for c in cs_:
                        tg, ta, tb, th = tgs[c % 4], tas[c], tbs[c], ths[c % 2]
                        KA, KB, KH = f"lr_ta{c}", f"lr_tb{c}", f"lr_th{c % 2}"
                        p.op("pool", lambda e, tb=tb, tg=tg: e.tensor_tensor(
                            out=tb[:, :To], in0=tb[:, :To], in1=tg[:, :To], op=ALU.mult),
                            reads=(KB, f"lr_tg{c % 4}"), writes=(KB,))
                        p.op("dve", lambda e, c=c, tb=tb: e.tensor_tensor(
                            out=tb[:, :To], in0=tb[:, :To], in1=xr[:, c, :To], op=ALU.mult),
                            reads=(KB, f"lr_xr{c}"), writes=(KB,))
                        if d == 0:
                            p.op("dve", lambda e, c=c, ta=ta, tb=tb: e.tensor_tensor_scan(
                                out=st[:, c, :To], data0=ta[:, :To], data1=tb[:, :To], initial=0.0,
                                op0=ALU.mult, op1=ALU.add),
                                reads=(KA, KB), writes=(f"lr_s{c}",))
                            p.op("dve", lambda e, c=c, ta=ta: e.tensor_tensor_scan(
                                out=pf[:, c, :To], data0=ta[:, :To], data1=zeros[:, :To], initial=1.0,
                                op0=ALU.mult, op1=ALU.add),
                                reads=(KA, "lr_zeros"), writes=(f"lr_pf{c}",))
                            p.op("pool", lambda e, c=c: e.tensor_copy(out=ea[:, 0, c, kidx:kidx + 1], in_=st[:, c, To - 1:To]),
                                 reads=(f"lr_s{c}",), writes=("lr_ea",))
                            p.op("pool", lambda e, c=c: e.tensor_copy(out=ea[:, 1, c, kidx:kidx + 1], in_=pf[:, c, To - 1:To]),
                                 reads=(f"lr_pf{c}",), writes=("lr_ea",))
                        else:
                            p.op("dve", lambda e, c=c, ta=ta, tb=tb, th=th: e.tensor_tensor_scan(
                                out=th[:, 0:To][:, ::-1], data0=ta[:, 0:To][:, ::-1],
                                data1=tb[:, 0:To][:, ::-1], initial=0.0, op0=ALU.mult, op1=ALU.add),
                                reads=(KA, KB), writes=(KH,))
                            p.op("dve", lambda e, c=c, ta=ta: e.tensor_tensor_scan(
                                out=pb[:, c, 0:To][:, ::-1], data0=ta[:, 0:To][:, ::-1], data1=zeros[:, :To],
                                initial=1.0, op0=ALU.mult, op1=ALU.add),
                                reads=(KA, "lr_zeros"), writes=(f"lr_pb{c}",))
                            p.op("pool", lambda e, c=c, th=th: e.tensor_copy(out=ea[:, 2, c, kidx:kidx + 1], in_=th[:, 0:1]),
                                 reads=(KH,), writes=("lr_ea",))
                            p.op("pool", lambda e, c=c: e.tensor_copy(out=ea[:, 3, c, kidx:kidx + 1], in_=pb[:, c, 0:1]),
                                 reads=(f"lr_pb{c}",), writes=("lr_ea",))
                            p.op("dve", lambda e, c=c, th=th: e.tensor_tensor(
                                out=st[:, c, :To], in0=st[:, c, :To], in1=th[:, :To], op=ALU.add),
                                reads=(f"lr_s{c}", KH), writes=(f"lr_s{c}",))
            for (buf, key, dd, nm) in ((st, "lr_s", outs[0], "s"), (pf, "lr_pf", outs[1], "pf"),
                                        (pb, "lr_pb", outs[2], "pb")):
                dv = dd.rearrange("(c p) t -> p c t", p=128)
                p.dma("sp", dv[:, :, d0:d0 + To], buf[:, :, :To], reads=tuple(f"{key}{c}" for c in range(NCH)),
                      stream="st_lr" + nm, final=True)
            dv = outs[3].rearrange("(c p) t -> p c t", p=128)
            p.dma("sp", dv[:, :, d0:d0 + To], gl[:, :, :To], reads=("lr_gl",), stream="st_lrgl", final=True)
        for t_ in tiles:
            body(*t_)
        p.dma("sp", ea_d[:, :], ea[:, :, :, :].rearrange("p a b c -> p (a b c)"), reads=("lr_ea",),
              stream="st_ea", final=True)


def lruB_stage(L, layer, w_out_d, tiles, ins, ea_own_d, ea_par_d, ntl, ntt, fA, fB):
    p = L.p
    RD = ("vecs", "vecs_der")
    with p.scope():
        wo = p.sbuf("lb_wo", [128, NCH, D], BF16)
        load_weight_bf16(p, wo, "lb_wo", w_out_d, NCH, D, "ld_lbwo")
        eo = p.sbuf("lb_eo", [128, 4, 8, ntt], F32)
        ep = p.sbuf("lb_ep", [128, 4, 8, ntt], F32)
        p.dma("sp", eo[:, :, :, :].rearrange("p a b c -> p (a b c)"), ea_own_d[:, :], writes=("lb_eo",), stream="ld_eo")
        ch = p.sbuf("lb_ch", [128, 8, 8], F32)
        cf = p.sbuf("lb_cf", [128, ntt + 1, 8], F32)
        cb = p.sbuf("lb_cb", [128, ntt + 1, 8], F32)
        K = ("lb_chain",)

        def step(out, E, A, st):
            p.op("dve", lambda e: e.tensor_tensor(out=ch[:, 4, :], in0=A, in1=st, op=ALU.mult),
                 reads=K + ("lb_eo", "lb_ep"), writes=K)
            p.op("dve", lambda e: e.tensor_tensor(out=out, in0=ch[:, 4, :], in1=E, op=ALU.add),
                 reads=K + ("lb_eo", "lb_ep"), writes=K)
        ctxF = eo[:, 0, :, ntl]
        ctxB = eo[:, 2, :, ntl]
        p.op("dve", lambda e: e.tensor_copy(out=cf[:, 0, :], in_=ctxF), reads=("lb_eo",), writes=K)
        p.op("dve", lambda e: e.tensor_copy(out=cb[:, ntl - 1, :], in_=ctxB), reads=("lb_eo",), writes=K)
        for k in range(ntl - 1):
            step(cf[:, k + 1, :], eo[:, 0, :, k], eo[:, 1, :, k], cf[:, k, :])
        for k in range(ntl - 1, 0, -1):
            step(cb[:, k - 1, :], eo[:, 2, :, k], eo[:, 3, :, k], cb[:, k, :])
        p.op("dve", lambda e: e.memset(cf[:, ntl, :], 0.0), reads=K, writes=K)
        p.op("dve", lambda e: e.memset(cb[:, ntl, :], 0.0), reads=K, writes=K)
        st = p.sbuf("lb_s", [128, NCH, 512], F32)
        pf = p.sbuf("lb_pf", [128, NCH, 512], F32)
        pb = p.sbuf("lb_pb", [128, NCH, 512], F32)
        gl = p.sbuf("lb_gl", [128, NCH, 512], F32)
        xt = p.sbuf("lb_xt", [128, NCH, 512], F32)
        y = p.sbuf("lb_y", [128, NCH, 512], BF16)
        sq = [p.sbuf(f"lb_sq{i}", [128, 512], F32) for i in range(2)]
        ps_o = [p.psum(f"lb_ps_o{i}", [128, 512]) for i in range(2)]
        bo = L.V("b_bo")
        g1s = {who: L.mod(layer, who, 2) for who in sorted(set(t[1] for t in tiles))}
        def body(To, who, d0, kidx, xsrc, xs0, xdst, xd0):
            for (buf, key, dd, nm) in ((st, "lb_s", ins[0], "s"), (pf, "lb_pf", ins[1], "pf"),
                                        (pb, "lb_pb", ins[2], "pb"), (gl, "lb_gl", ins[3], "gl")):
                dv = dd.rearrange("(c p) t -> p c t", p=128)
                p.dma("sp", buf[:, :, :To], dv[:, :, d0:d0 + To], writes=(key,), stream="ld_lb" + nm)
            xv = xsrc.rearrange("(c p) t -> p c t", p=128)
            p.dma("sp", xt[:, :, :To], xv[:, :, xs0:xs0 + To], writes=("lb_xt",), stream="ld_lbx")
            for c in range(NCH):
                s = sq[c % 2]
                sk = f"lb_sq{c % 2}"
                p.op("dve", lambda e, c=c, s=s: e.scalar_tensor_tensor(
                    out=s[:, :To], in0=pf[:, c, :To], scalar=cf[:, kidx, c:c + 1], in1=st[:, c, :To],
                    op0=ALU.mult, op1=ALU.add), reads=("lb_pf", "lb_s") + K, writes=(sk,))
                p.op("dve", lambda e, c=c, s=s: e.scalar_tensor_tensor(
                    out=s[:, :To], in0=pb[:, c, :To], scalar=cb[:, kidx, c:c + 1], in1=s[:, :To],
                    op0=ALU.mult, op1=ALU.add), reads=("lb_pb", sk) + K, writes=(sk,))
                p.op("dve", lambda e, c=c, s=s: e.tensor_tensor(
                    out=y[:, c, :To], in0=s[:, :To], in1=gl[:, c, :To], op=ALU.mult),
                    reads=(sk, "lb_gl"), writes=(f"lb_y{c}",))
            for n in range(NCH):
                po = ps_o[n % 2]
                for k in range(NCH):
                    p.op("pe", lambda e, n=n, k=k, po=po: e.matmul(
                        po[:, :To], wo[:, k, n * 128:(n + 1) * 128], y[:, k, :To],
                        start=(k == 0), stop=(k == NCH - 1)),
                        reads=("lb_wo", f"lb_y{k}"), writes=(f"lb_ps_o{n % 2}",))
                s = sq[n % 2]
                sk = f"lb_sq{n % 2}"
                p.op("act", lambda e, n=n, s=s, po=po: e.activation(
                    out=s[:, :To], in_=po[:, :To], func=AF.Identity, bias=bo[:, n:n + 1], scale=1.0),
                    reads=(f"lb_ps_o{n % 2}",) + RD, writes=(sk,))
                g1 = g1s[who]
                p.op("dve", lambda e, n=n, s=s, g1=g1: e.scalar_tensor_tensor(
                    out=xt[:, n, :To], in0=s[:, :To], scalar=g1[:, n:n + 1], in1=xt[:, n, :To],
                    op0=ALU.mult, op1=ALU.add), reads=(sk, "lb_xt") + RD, writes=("lb_xt",))
            dv = xdst.rearrange("(c p) t -> p c t", p=128)
            p.dma("sp", dv[:, :, xd0:xd0 + To], xt[:, :, :To], reads=("lb_xt",), stream="st_lbx", final=True)
        for t_ in tiles:
            body(*t_)


def qkv_stage(L, layer, w_qkv_d, tiles, q_d, k_d, v_d, bvrow_d):
    p = L.p
    RD = ("vecs", "vecs_der")
    with p.scope():
        w = p.sbuf("qk_w", [128, NCH, 3 * D], BF16)
        load_weight_bf16(p, w, "qk_w", w_qkv_d, NCH, 3 * D, "ld_qkw")
        xt = p.sbuf("qk_xt", [128, NCH, 512], F32)
        h = p.sbuf("qk_h", [128, NCH, 512], BF16)
        o = [p.sbuf(f"qk_o{i}", [128, NCH, 512], BF16) for i in range(2)]
        vt = p.sbuf("qk_vt", [128, D], BF16)
        bvr = p.sbuf("qk_bvr", [128, D], F32)
        p.dma("sp", bvr[:, :], bvrow_d[:, :], writes=("qk_bvr",), stream="ld_bvr")
        sq = [p.sbuf(f"qk_sq{i}", [128, 512], F32) for i in range(2)]
        rstd = p.sbuf("qk_rstd", [128, 512], F32)
        ps_ss = p.psum("qk_ps_ss", [128, 512])
        ps = [p.psum(f"qk_ps{i}", [128, 512]) for i in range(2)]
        bq = L.V("c_bqkv")
        bq8 = L.derived(lambda e, o_: e.tensor_scalar(out=o_, in0=bq[:, 0:8], scalar1=0.125, scalar2=None, op0=ALU.mult))
        mods = {who: (L.gscale(layer, who, 0), L.mod(layer, who, 0)) for who in sorted(set(t[3] for t in tiles))}
        def body(src, s0, T, who, d0):
            gs, sh = mods[who]
            sv = src.rearrange("(c p) t -> p c t", p=128)
            p.dma("sp", xt[:, :, :T], sv[:, :, s0:s0 + T], writes=("qk_xt",), stream="ld_qkx")
            rmsnorm_mod(p, "qk_", xt, "qk_xt", T, sq, rstd, sq, ps_ss, p.ones, h, "qk_h", gs, sh, None)
            for m in range(2 * NCH):
                t3, c = divmod(m, NCH)
                pp = ps[m % 2]
                for k in range(NCH):
                    p.op("pe", lambda e, m=m, k=k, pp=pp: e.matmul(
                        pp[:, :T], w[:, k, m * 128:(m + 1) * 128], h[:, k, :T],
                        start=(k == 0), stop=(k == NCH - 1)),
                        reads=("qk_w", "qk_h"), writes=(f"qk_ps{m % 2}",))
                if t3 == 0:
                    p.op("act", lambda e, c=c, pp=pp: e.activation(
                        out=o[0][:, c, :T], in_=pp[:, :T], func=AF.Identity, bias=bq8[:, c:c + 1], scale=0.125),
                        reads=(f"qk_ps{m % 2}",) + RD, writes=("qk_o0",))
                else:
                    p.op("act", lambda e, c=c, pp=pp, t3=t3, m=m: e.activation(
                        out=o[t3][:, c, :T], in_=pp[:, :T], func=AF.Identity, bias=bq[:, m:m + 1], scale=1.0),
                        reads=(f"qk_ps{m % 2}",) + RD, writes=(f"qk_o{t3}",))
            for t3, dd in enumerate((q_d, k_d)):
                dv = dd.rearrange("(c p) t -> p c t", p=128)
                p.dma("sp", dv[:, :, d0:d0 + T], o[t3][:, :, :T], reads=(f"qk_o{t3}",), stream=f"st_qk{t3}", final=True)
            for tb in range((T + 127) // 128):
                m_ = min(128, T - tb * 128)
                for cg in range(2):
                    pp = ps[cg]
                    for k in range(NCH):
                        p.op("pe", lambda e, tb=tb, m_=m_, cg=cg, k=k, pp=pp: e.matmul(
                            pp[:m_, :512], h[:, k, tb * 128:tb * 128 + m_],
                            w[:, k, 2 * D + cg * 512:2 * D + (cg + 1) * 512],
                            start=(k == 0), stop=(k == NCH - 1)),
                            reads=("qk_w", "qk_h"), writes=(f"qk_ps{cg}",))
                    p.op("dve", lambda e, m_=m_, cg=cg, pp=pp: e.tensor_tensor(
                        out=vt[:m_, cg * 512:(cg + 1) * 512], in0=pp[:m_, :512],
                        in1=bvr[:m_, cg * 512:(cg + 1) * 512], op=ALU.add),
                        reads=(f"qk_ps{cg}", "qk_bvr"), writes=("qk_vt",))
                p.dma("sp", v_d[d0 + tb * 128:d0 + tb * 128 + m_, :], vt[:m_, :], reads=("qk_vt",),
                      stream="st_qkv", final=True)
        for t_ in tiles:
            body(*t_)


AX = mybir.AxisListType.X
NROWS = 64
GW = 64


def na_stage(L, layer, w_o_d, q_d, k_d, vtok_d, coff, tabi_d, tabb_d, identb_d, x_src, x_dst, nqrows):
    p = L.p
    RD = ("vecs", "vecs_der")
    with p.scope():
        wo = p.sbuf("na_wo", [128, NCH, D], BF16)
        load_weight_bf16(p, wo, "na_wo", w_o_d, NCH, D, "ld_nawo")
        idb = p.sbuf("na_idb", [128, 128], BF16)
        p.dma("pool", idb[:, :], identb_d[:, :], writes=("na_idb",), stream="ld_idb")
        tabi = p.sbuf("na_tabi", [128, 8, 576], F32)
        p.dma("sp", tabi[:, :, :], tabi_d[:, :, :], writes=("na_tabi",), stream="ld_tabi")
        tabb = p.sbuf("na_tabb", [128, 8, 576], F32)
        kcb = p.sbuf("na_kcb", [128, NCH, CTX], BF16)
        vcb = p.sbuf("na_vcb", [64, 4, D], BF16)
        p.dma("sp", kcb[:, :, :], k_d.rearrange("(c p) t -> p c t", p=128)[:, :, coff:coff + CTX], writes=("na_kcb",),
              stream="ld_kcb")
        p.dma("sp", vcb[:, :, :], vtok_d[coff:coff + CTX, :].rearrange("(r c) d -> c r d", c=64), writes=("na_vcb",),
              stream="ld_vcb")
        kb = p.sbuf("na_kb", [128, NCH, 16 * GW], BF16)
        vb = p.sbuf("na_vb", [64, 16, D], BF16)
        qb = p.sbuf("na_qb", [128, NCH, 256], BF16)
        ofm = p.sbuf("na_ofm", [128, NCH, 256], BF16)
        xt = p.sbuf("na_xt", [128, NCH, 256], F32)
        sl = [p.sbuf(f"na_sl{i}", [128, 576], F32) for i in range(2)]
        pl = [p.sbuf(f"na_pl{i}", [128, 1024], BF16) for i in range(2)]
        pT = [p.sbuf(f"na_pT{i}", [64, 16, 128], BF16) for i in range(2)]
        sm = [p.sbuf(f"na_sm{i}", [128, 8], F32) for i in range(2)]
        tmp = p.sbuf("na_tmp", [128, 256], F32)
        ps_s = [p.psum(f"na_ps_s{i}", [128, 512]) for i in range(2)]
        ps_sc = [p.psum(f"na_ps_sc{i}", [128, 512]) for i in range(2)]
        ps_t = p.psum("na_ps_t", [128, 16, 128], BF16)
        ps_o = p.psum("na_ps_o", [128, 512])
        ps_y = p.psum("na_ps_y", [128, 512])
        bo = L.V("c_bo")
        g1 = L.mod(layer, 0, 2)
        khv = k_d.rearrange("(c p) t -> p c t", p=128)
        qv = q_d.rearrange("(c p) t -> p c t", p=128)
        xv = x_src.rearrange("(c p) t -> p c t", p=128)
        dv = x_dst.rearrange("(c p) t -> p c t", p=128)
        it = 0
        blocks = [(4 * b_, 4) for b_ in range(nqrows // 4)]
        if nqrows % 4:
            blocks.append((4 * (nqrows // 4), nqrows % 4))
        for (i0, nr) in blocks:
            base = max(i0 - 4, 0)
            nrows = max(i0 + nr - 5, 0) + 9 - base
            nkr = 9
            nk = nkr * 64
            nblk = nkr + 4
            nt = nr * 64
            p.dma("sp", kb[:, :, :nrows * GW], khv[:, :, base * GW:(base + nrows) * GW], writes=("na_kb",), stream="ld_kb")
            p.dma("sp", vb[:, :nrows, :], vtok_d[base * GW:(base + nrows) * GW, :].rearrange("(r c) d -> c r d", c=64),
                  writes=("na_vb",), stream="ld_vb")
            p.dma("sp", qb[:, :, :nt], qv[:, :, i0 * 64:i0 * 64 + nt], writes=("na_qb",), stream="ld_qb")
            p.dma("sp", xt[:, :, :nt], xv[:, :, i0 * 64:i0 * 64 + nt], writes=("na_xt",), stream="ld_nax")
            def front(ii, hp, par, so, tab, tabkey):
                s_, p_, m_ = sl[par], pl[par], sm[par]
                ks, kp, km = f"na_sl{par}", f"na_pl{par}", f"na_sm{par}"
                pss, psc = ps_s[par], ps_sc[par]
                kpss, kpsc = f"na_ps_s{par}", f"na_ps_sc{par}"
                for hh in range(2):
                    pr = slice(hh * 64, (hh + 1) * 64)
                    p.op("pe", lambda e, pr=pr: e.matmul(
                        pss[pr, :512], qb[pr, hp, ii * 64:(ii + 1) * 64],
                        kb[pr, hp, so * 64:so * 64 + 512], start=True, stop=True),
                        reads=("na_qb", "na_kb"), writes=(kpss,))
                    p.op("pe", lambda e, pr=pr: e.matmul(
                        psc[pr, 0:64], qb[pr, hp, ii * 64:(ii + 1) * 64],
                        kb[pr, hp, so * 64 + 512:so * 64 + 576], start=True, stop=True),
                        reads=("na_qb", "na_kb"), writes=(kpsc,))
                    p.op("pe", lambda e, pr=pr: e.matmul(
                        psc[pr, 64:64 + CTX], qb[pr, hp, ii * 64:(ii + 1) * 64], kcb[pr, hp, :],
                        start=True, stop=True, skip_group_check=True),
                        reads=("na_qb", "na_kcb"), writes=(kpsc,))
                p.op("dve", lambda e: e.tensor_tensor(
                    out=s_[:, 0:512], in0=pss[:, :512], in1=tab[:, hp, 0:512], op=ALU.add),
                    reads=(kpss, tabkey), writes=(ks,))
                p.op("dve", lambda e: e.tensor_tensor(
                    out=s_[:, 512:576], in0=psc[:, 0:64], in1=tab[:, hp, 512:576], op=ALU.add),
                    reads=(kpsc, tabkey), writes=(ks,))
                p.op("dve", lambda e: e.reduce_max(out=m_[:, 0:1], in_=s_[:, :576], axis=AX),
                     reads=(ks,), writes=(km,))
                p.op("dve", lambda e: e.reduce_max(out=m_[:, 1:2], in_=psc[:, 64:64 + CTX], axis=AX),
                     reads=(kpsc,), writes=(km,))
                p.op("dve", lambda e: e.tensor_tensor(out=m_[:, 2:3], in0=m_[:, 0:1], in1=m_[:, 1:2], op=ALU.max),
                     reads=(km,), writes=(km,))
                p.op("dve", lambda e: e.tensor_scalar(out=m_[:, 3:4], in0=m_[:, 2:3], scalar1=-1.0, scalar2=None,
                                                      op0=ALU.mult),
                     reads=(km,), writes=(km,))
                p.op("act", lambda e: e.activation(
                    out=p_[:, :576], in_=s_[:, :576], func=AF.Exp, bias=m_[:, 3:4], scale=1.0, accum_out=m_[:, 4:5]),
                    reads=(ks, km), writes=(kp, km + "a"))
                p.op("act", lambda e: e.activation(
                    out=p_[:, 576:576 + CTX], in_=psc[:, 64:64 + CTX], func=AF.Exp, bias=m_[:, 3:4], scale=1.0,
                    accum_out=m_[:, 5:6]),
                    reads=(kpsc, km), writes=(kp, km + "a"))
                p.op("dve", lambda e: e.tensor_tensor(out=m_[:, 6:7], in0=m_[:, 4:5], in1=m_[:, 5:6], op=ALU.add),
                     reads=(km + "a",), writes=(km + "b",))
                p.op("dve", lambda e: e.reciprocal(out=m_[:, 7:8], in_=m_[:, 6:7]),
                     reads=(km + "b",), writes=(km + "b",))
                p.op("dve", lambda e: e.tensor_scalar(
                    out=p_[:, :576 + CTX], in0=p_[:, :576 + CTX], scalar1=m_[:, 7:8], scalar2=None, op0=ALU.mult),
                    reads=(kp, km + "b"), writes=(kp,))

            def back(ii, hp, par, so):
                p_, t_ = pl[par], pT[par]
                kp, kt = f"na_pl{par}", f"na_pT{par}"
                for j in range(nblk):
                    p.op("pe", lambda e, j=j: e.transpose(
                        out=ps_t[:64, j, :], in_=p_[:, j * 64:(j + 1) * 64], identity=idb[:, :]),
                        reads=(kp, "na_idb"), writes=("na_ps_t",))
                p.op("act", lambda e: e.activation(
                    out=t_[:, :nblk, :], in_=ps_t[:64, :nblk, :], func=AF.Identity, bias=p.zero_t[:64, 0:1], scale=1.0),
                    reads=("na_ps_t", "zeroc"), writes=(kt,))

            def back2(ii, hp, par, so):
                t_ = pT[par]
                kt = f"na_pT{par}"
                for hh in range(2):
                    h_ = 2 * hp + hh
                    for j in range(nblk):
                        if j < nkr:
                            lhs = vb[:, so + j, h_ * 64:(h_ + 1) * 64]
                        else:
                            lhs = vcb[:, j - nkr, h_ * 64:(h_ + 1) * 64]
                        p.op("pe", lambda e, hh=hh, j=j, lhs=lhs: e.matmul(
                            ps_o[hh * 64:(hh + 1) * 64, ii * 64:(ii + 1) * 64], lhs, t_[:, j, hh * 64:(hh + 1) * 64],
                            start=(j == 0), stop=(j == nblk - 1)),
                            reads=("na_vb", "na_vcb", kt), writes=(f"na_ps_o{ii}",))
                p.op("act", lambda e: e.activation(
                    out=ofm[:, hp, ii * 64:(ii + 1) * 64], in_=ps_o[:, ii * 64:(ii + 1) * 64], func=AF.Identity,
                    bias=p.zero_t[:, 0:1], scale=1.0),
                    reads=(f"na_ps_o{ii}", "zeroc"), writes=("na_ofm",))

            work = []
            for ii in range(nr):
                i = i0 + ii
                so = max(i - 4, 0) - base
                if i < 5:
                    tab, tabkey = tabb, "na_tabb"
                else:
                    tab, tabkey = tabi, "na_tabi"
                for hp in range(8):
                    work.append((ii, hp, it % 2, so, tab, tabkey, i if (i < 5 and hp == 0) else None))
                    it += 1
            nw = len(work)
            for n_ in range(nw + 2):
                if n_ < nw:
                    wk = work[n_]
                    if wk[6] is not None:
                        p.dma("sp", tabb[:, :, :], tabb_d[wk[6]], writes=("na_tabb",), stream="ld_tabb")
                    front(*wk[:6])
                if 1 <= n_ <= nw:
                    back(*work[n_ - 1][:4])
                if 2 <= n_:
                    back2(*work[n_ - 2][:4])
            for n in range(NCH):
                for k in range(NCH):
                    p.op("pe", lambda e, n=n, k=k, nt=nt: e.matmul(
                        ps_y[:, :nt], wo[:, k, n * 128:(n + 1) * 128], ofm[:, k, :nt],
                        start=(k == 0), stop=(k == NCH - 1)),
                        reads=("na_wo", "na_ofm"), writes=("na_ps_y",))
                p.op("act", lambda e, n=n, nt=nt: e.activation(out=tmp[:, :nt], in_=ps_y[:, :nt], func=AF.Identity,
                                                        bias=bo[:, n:n + 1], scale=1.0),
                     reads=("na_ps_y",) + RD, writes=("na_tmp",))
                p.op("dve", lambda e, n=n, nt=nt: e.scalar_tensor_tensor(
                    out=xt[:, n, :nt], in0=tmp[:, :nt], scalar=g1[:, n:n + 1], in1=xt[:, n, :nt],
                    op0=ALU.mult, op1=ALU.add), reads=("na_tmp", "na_xt") + RD, writes=("na_xt",))
            p.dma("sp", dv[:, :, i0 * 64:i0 * 64 + nt], xt[:, :, :nt], reads=("na_xt",), stream="st_nax", final=True)


def final_norm_stage(L, tiles):
    p = L.p
    with p.scope():
        xt = p.sbuf("fn_xt", [128, NCH, 512], F32)
        ho = p.sbuf("fn_h", [128, NCH, 512], F32)
        sq = [p.sbuf(f"fn_sq{i}", [128, 512], F32) for i in range(2)]
        rstd = p.sbuf("fn_rstd", [128, 512], F32)
        ps_ss = p.psum("fn_ps_ss", [128, 512])
        fg = L.V("final_g")
        zer = L.V("zeros8")
        for (src, s0, dst, d0, T) in tiles:
            sv = src.rearrange("(c p) t -> p c t", p=128)
            dv = dst.rearrange("(c p) t -> p c t", p=128)
            p.dma("sp", xt[:, :, :T], sv[:, :, s0:s0 + T], writes=("fn_xt",), stream="ld_fnx")
            rmsnorm_mod(p, "fn_", xt, "fn_xt", T, sq, rstd, sq, ps_ss, p.ones, ho, "fn_h", fg, zer, None)
            p.dma("sp", dv[:, :, d0:d0 + T], ho[:, :, :T], reads=("fn_h",), stream="st_fn", final=True)


def mod_stage(L, cv, bm, w_mod):
    p = L.p
    with p.scope():
        cvt = p.sbuf("cvt", [128, 8, 2], F32)
        bmt = p.sbuf("bmt", [128, 4 * 48], F32)
        wb = [p.sbuf(f"wb{i}", [128, 8, 768], BF16) for i in range(3)]
        cvb = p.sbuf("cvb", [128, 8, 2], BF16)
        ps = [p.psum(f"psm{i}", [128, 48, 2]) for i in range(2)]
        p.dma("sp", cvt[:, :, :], cv.rearrange("p (k w) -> p k w", w=2), writes=("cv",), stream="ld_cv")
        p.dma("sp", bmt[:, :], bm[:, :], writes=("bm",), stream="ld_bm")
        p.op("act", lambda e: e.activation(out=cvb[:, :, :], in_=cvt[:, :, :], func=AF.Silu),
             reads=("cv",), writes=("cvb",))
        n = 0
        for i in range(4):
            wv = w_mod[i].rearrange("(c p) n -> p c n", p=128)
            for pc in range(8):
                b = wb[n % 3]
                p.dma("pool", b[:, :, :], wv[:, :, pc * 768:(pc + 1) * 768], writes=(f"wb{n % 3}",),
                      stream=f"ld_wb{n % 3}")
                for jj in range(6):
                    j = pc * 6 + jj
                    for kc in range(8):
                        p.op("pe", lambda e, b=b, jj=jj, j=j, kc=kc, i=i: e.matmul(
                            ps[i % 2][:, j, :], b[:, kc, jj * 128:(jj + 1) * 128], cvb[:, kc, :],
                            start=(kc == 0), stop=(kc == 7)),
                            reads=(f"wb{n % 3}", "cvb"), writes=(f"psm{i % 2}",))
                n += 1
            for who in range(2):
                o0 = (i * 2 + who) * 48
                p.op("dve", lambda e, i=i, who=who, o0=o0: e.tensor_tensor(
                    out=L.mt[:, o0:o0 + 48], in0=ps[i % 2][:, :, who], in1=bmt[:, i * 48:(i + 1) * 48],
                    op=ALU.add), reads=(f"psm{i % 2}", "bm"), writes=("vecs",))


SF = SEQ
NTL = len(tok_tiles(SF, LRU_TO))
NTT = NTL + 1
NB1 = 9
R1 = NB1 * LRU_TO
NQR = 65
R2 = NQR * GW
W = SEQ // 2
NEG = -30000.0


def _taps(w):
    K = w.shape[0]
    return np.ascontiguousarray(np.asarray(w, np.float32).reshape(K, 8, 128).transpose(2, 0, 1).reshape(128, K * 8))


def _vec_spec():
    spec = [(f"ng{i}_{w}", 8) for i in range(4) for w in range(2)]
    spec += [("zeros8", 8), ("final_g", 8)]
    for j in range(2):
        spec += [(f"a{j}_b1", 16), (f"a{j}_wdw", 248), (f"a{j}_bdw", 8), (f"a{j}_lng", 8), (f"a{j}_lnb", 8), (f"a{j}_b2", 8)]
    spec += [("b_bin", 16), ("b_wc", 40), ("b_bcv", 8), ("b_brg", 16), ("b_big", 16), ("b_lam", 16), ("b_bo", 8),
             ("c_bqkv", 24), ("c_bo", 8)]
    return spec


def _vecs_host(I, hf):
    parts = [fm(I["norm_g"][i, w]) for i in range(4) for w in range(2)]
    parts += [np.zeros((128, 8), np.float32), fm(I["final_g"])]
    for j in range(2):
        wdw = I["a_w_dw"][j] if hf == 0 else I["a_w_dw"][j][::-1]
        parts += [fm(I["a_b_pw1"][j]), _taps(wdw), fm(I["a_b_dw"][j]), fm(I["a_ln_g"][j]),
                  fm(I["a_ln_b"][j]), fm(I["a_b_pw2"][j])]
    wc = np.asarray(I["b_w_conv"][0], np.float32)
    z = np.zeros((1, D), np.float32)
    wc5 = np.concatenate([wc, z], 0) if hf == 0 else np.concatenate([z, wc[::-1]], 0)
    dsel = slice(None) if hf == 0 else slice(None, None, -1)
    parts += [fm(I["b_b_in"][0]), _taps(wc5), fm(I["b_b_conv"][0]), fm(I["b_b_rg"][0][dsel].reshape(-1)),
              fm(I["b_b_ig"][0][dsel].reshape(-1)), fm(I["b_lam"][0][dsel].reshape(-1)), fm(I["b_b_out"][0]),
              fm(I["c_b_qkv"][0]), fm(I["c_b_o"][0])]
    return np.ascontiguousarray(np.concatenate(parts, axis=1).astype(np.float32))


def _na_tables(rpb, hf):
    rpb = np.asarray(rpb, np.float32)
    ql = np.arange(64)
    q = ql if hf == 0 else 63 - ql
    kc = ql if hf == 0 else 63 - ql
    cs = np.clip(q - 8, 0, 48)
    colok = (kc[None, :] >= cs[:, None]) & (kc[None, :] < cs[:, None] + 16)
    coff = np.clip(kc[None, :] - q[:, None] + 15, 0, 30)

    def table(i):
        t = np.full((2, 64, 8, 9, 64), NEG, np.float32)
        r = i if hf == 0 else 127 - i
        r0 = int(np.clip(r - 4, 0, 120))
        for j in range(9):
            krl = max(i - 4, 0) + j
            kr = krl if hf == 0 else 127 - krl
            if r0 <= kr < r0 + 8:
                vals = rpb[:, kr - r + 7][:, coff]
                vals = np.where(colok[None], vals, NEG)
                t[:, :, :, j, :] = vals.reshape(8, 2, 64, 64).transpose(1, 2, 0, 3)
        return t.reshape(128, 8, 9 * 64)
    tabi = table(30)
    tabb = np.stack([table(i) for i in range(5)], 0)
    return np.ascontiguousarray(tabi), np.ascontiguousarray(tabb)


def build_fused():
    L = Launch()
    L.declare_vecs(_vec_spec())
    nc = L.nc
    p = L.p

    def internal(name, shape, dt=F32):
        return nc.dram_tensor(name, list(shape), dt, kind="Internal").ap()
    cv = L.inp("cv", [128, 16]); bm = L.inp("bm", [128, 4 * 48]); w_mod = L.inp("w_mod", [4, D, 6 * D])
    xh = L.inp("xh", [D, SF + 30]); ch = L.inp("ch", [D, CTX + 30]); idd = L.inp("ident", [128, 128])
    a_w1 = L.inp("a_w1", [2, D, 2 * D]); a_w2 = L.inp("a_w2", [2, D, D])
    wfi = L.inp("wfi", [4, D, 2 * DFF]); wfo = L.inp("wfo", [4, DFF, D])
    win = L.inp("win", [D, 2 * D]); wrg = L.inp("wrg", [2, 8, 128, 128]); wig = L.inp("wig", [2, 8, 128, 128])
    wout = L.inp("wout", [D, D]); wqkv = L.inp("wqkv", [D, 3 * D]); wo = L.inp("wo", [D, D])
    bvrow = L.inp("bvrow", [128, D]); tid = L.inp("tabi", [128, 8, 576]); tbd = L.inp("tabb", [5, 128, 8, 576])
    od = L.out("o", [D, W])
    xm0 = internal("xm0", [D, SF]); cm0 = internal("cm0", [D, CTX])
    x0 = internal("x0", [D, SF + 4]); c0 = internal("c0", [D, CTX + 4])
    lr = [internal("lrd_" + n, [D, SF + CTX]) for n in ("s", "pf", "pb", "gl")]
    ea = internal("ea", [128, 4 * 8 * NTT])
    xm1 = internal("xm1", [D, R1]); x1 = internal("x1", [D, R1])
    cm1 = internal("cm1", [D, CTX]); c1 = internal("c1", [D, CTX])
    qd = internal("q", [D, R1 + CTX], BF16); kd = internal("k", [D, R1 + CTX], BF16)
    vtok = internal("vtok", [R1 + CTX, D], BF16)
    xm2 = internal("xm2", [D, R2]); x2h = internal("x2h", [D, 15 + R2])
    xm3 = internal("xm3", [D, W]); x3 = internal("x3", [D, W])
    L.preamble(fused=True)
    zc = p.zero_t[:, 0:1]
    zt = p.sbuf("zpad", [128, NCH, 16], F32)
    p.op("dve", lambda e: e.memset(zt[:, :, :], 0.0), writes=("zpad",))
    for (t, c0_, n) in ((x0, 0, 2), (x0, SF + 2, 2), (c0, 0, 2), (c0, CTX + 2, 2), (x2h, 0, 15)):
        p.dma("sp", t.rearrange("(c p) t -> p c t", p=128)[:, :, c0_:c0_ + n], zt[:, :, :n], reads=("zpad",),
              stream="st_zpad", final=True)
    mod_stage(L, cv, bm, w_mod)
    lat = tok_tiles(SF, 512)
    ct = tok_tiles(SF, CONV_TO)
    tiles = [(xh, t0, xm0, t0, To, 0, zc if i == 0 else None, zc if i == len(ct) - 1 else None)
             for i, (t0, To) in enumerate(ct)]
    tiles.append((ch, 0, cm0, 0, CTX, 1, zc, zc))
    conv_stage(L, 0, 0, a_w1[0], a_w2[0], idd, tiles)
    ffn_stage(L, 0, wfi[0], wfo[0], [(xm0, t0, x0, 2 + t0, T, 0) for (t0, T) in lat] + [(cm0, 0, c0, 2, CTX, 1)])
    lt = tok_tiles(SF, LRU_TO)
    tiles = [(x0, t0, To, 0, zc if i == 0 else None, zc if i == len(lt) - 1 else None, t0, i)
             for i, (t0, To) in enumerate(lt)]
    tiles.append((c0, 0, CTX, 1, zc, zc, SF, NTL))
    lruA_stage(L, 1, win, wrg, wig, tiles, lr, ea, NTT)
    tiles = [(To, 0, t0, i, x0, 2 + t0, xm1, t0) for i, (t0, To) in enumerate(lt[:NB1])]
    tiles.append((CTX, 1, SF, NTL, c0, 2, cm1, 0))
    lruB_stage(L, 1, wout, tiles, lr, ea, None, NTL, NTT, None, None)
    l1 = tok_tiles(R1, 512)
    ffn_stage(L, 1, wfi[1], wfo[1], [(xm1, t0, x1, t0, T, 0) for (t0, T) in l1] + [(cm1, 0, c1, 0, CTX, 1)])
    qkv_stage(L, 2, wqkv, [(x1, t0, T, 0, t0) for (t0, T) in l1] + [(c1, 0, CTX, 1, R1)], qd, kd, vtok, bvrow)
    na_stage(L, 2, wo, qd, kd, vtok, R1, tid, tbd, idd, x1, xm2, NQR)
    ffn_stage(L, 2, wfi[2], wfo[2], [(xm2, t0, x2h, 15 + t0, T, 0) for (t0, T) in tok_tiles(R2, 512)])
    ct = tok_tiles(W, CONV_TO)
    tiles = [(x2h, t0, xm3, t0, To, 0, zc if i == 0 else None, None) for i, (t0, To) in enumerate(ct)]
    conv_stage(L, 3, 1, a_w1[1], a_w2[1], idd, tiles)
    lw = tok_tiles(W, 512)
    ffn_stage(L, 3, wfi[3], wfo[3], [(xm3, t0, x3, t0, T, 0) for (t0, T) in lw])
    final_norm_stage(L, [(x3, t0, od, t0, T) for (t0, T) in lw])
    return L


def kernel(**I):
    I = {k: np.asarray(v) for k, v in I.items()}
    cores = [(b, hf) for b in range(BATCH) for hf in range(2)]
    L = build_fused()
    ident = np.eye(128, dtype=np.float32)
    bm = np.ascontiguousarray(np.concatenate([fm(I["b_mod"][i]) for i in range(4)], axis=1))
    bvrow = np.ascontiguousarray(np.tile(np.asarray(I["c_b_qkv"][0][2 * D:], np.float32)[None, :], (128, 1)))
    vecs = {hf: _vecs_host(I, hf) for hf in range(2)}
    tabs = {hf: _na_tables(I["c_rpb"][0], hf) for hf in range(2)}
    wrg = {0: I["b_w_rg"][0], 1: np.ascontiguousarray(I["b_w_rg"][0][::-1])}
    wig = {0: I["b_w_ig"][0], 1: np.ascontiguousarray(I["b_w_ig"][0][::-1])}
    ims = []
    for (b, hf) in cores:
        xs = I["x"][b] if hf == 0 else I["x"][b][::-1]
        cs = I["ctx"][b] if hf == 0 else I["ctx"][b][::-1]
        cv = np.stack([fm(I["c"][b]), fm(I["c_ctx"])], axis=2).reshape(128, 16)
        ims.append({
            "vecs": vecs[hf], "cv": np.ascontiguousarray(cv), "bm": bm, "w_mod": I["w_mod"],
            "xh": _halo(np.ascontiguousarray(xs.T), 0, 15, 15, SF), "ch": _halo(np.ascontiguousarray(cs.T), 0, 15, 15, CTX),
            "ident": ident, "a_w1": I["a_w_pw1"], "a_w2": I["a_w_pw2"], "wfi": I["w_ffn_in"], "wfo": I["w_ffn_out"],
            "win": I["b_w_in"][0], "wrg": wrg[hf], "wig": wig[hf], "wout": I["b_w_out"][0], "wqkv": I["c_w_qkv"][0],
            "wo": I["c_w_o"][0], "bvrow": bvrow, "tabi": tabs[hf][0], "tabb": tabs[hf][1]})
    res = run_launch(L, ims)
    out = np.empty((BATCH, SEQ, D), np.float32)
    for n, (b, hf) in enumerate(cores):
        o = res[n]["o"].T
        if hf == 0:
            out[b, :W, :] = o
        else:
            out[b, W:, :] = o[::-1]
    return out


def _halo(full, t0, left, right, width):
    Dd, S = full.shape
    out = np.zeros((Dd, left + width + right), full.dtype)
    a, b = t0 - left, t0 + width + right
    aa, bb = max(a, 0), min(b, S)
    out[:, aa - a:bb - a] = full[:, aa:bb]
    return out
```

```python
import contextlib
import numpy as np
import concourse.bass as bass
import concourse.mybir as mybir
from concourse.bass_utils import run_bass_kernel_spmd

F32 = mybir.dt.float32
BF16 = mybir.dt.bfloat16
AF = mybir.ActivationFunctionType
ALU = mybir.AluOpType

D = 1024
NCH = 8
DFF = 2816
NFF = 22
SEQ = 8192
BATCH = 4
CTX = 256
EPS = 1e-6
NCORES = 8

COMPUTE = ("pe", "act", "dve", "pool")
SAME_ENGINE_SYNC = True
NO_SELF_SYNC = ("pe", "act")


class Prog:
    def __init__(self, nc):
        self.nc = nc
        self.ops = {e: [] for e in ("pe", "act", "dve", "pool", "sp")}
        self.cnt = {}
        self.waited = {e: {} for e in self.ops}
        self.keys = {}
        self.stack = contextlib.ExitStack()
        self.sems = {}
        self.dma_streams = []
        self.final = []
        self.semstack = contextlib.ExitStack()
        self.epoch = 0

    def sbuf(self, name, shape, dt):
        self.uid = getattr(self, "uid", 0) + 1
        return self.stack.enter_context(self.nc.sbuf_tensor(f"{name}_u{self.uid}", list(shape), dt))

    def psum(self, name, shape, dt=F32):
        self.uid = getattr(self, "uid", 0) + 1
        return self.stack.enter_context(self.nc.psum_tensor(f"{name}_u{self.uid}", list(shape), dt))

    def _sem(self, key):
        if key not in self.sems:
            self.sems[key] = self.semstack.enter_context(
                self.nc.semaphore("s_" + "".join(ch for ch in str(key) if ch.isalnum() or ch == "_")))
            self.cnt[key] = 0
        return self.sems[key]

    def barrier(self):
        for e in self.ops:
            waits = []
            for k, v in self.cnt.items():
                if v > 0 and self.waited[e].get(k, 0) < v:
                    self.waited[e][k] = v
                    waits.append((k, v))
            if waits:
                self.ops[e].append((waits, None, None))

    @contextlib.contextmanager
    def scope(self):
        outer = self.stack
        self.stack = contextlib.ExitStack()
        try:
            yield
            self.barrier()
            self.epoch += 1
        finally:
            self.stack.close()
            self.stack = outer

    def _deps(self, eng, reads, writes, selfsync=False):
        deps = {}

        def add(d):
            if d is None:
                return
            k, v = d
            if deps.get(k, 0) < v:
                deps[k] = v
        for k in reads:
            st = self.keys.get(k)
            if st is not None:
                add(st[0])
        for k in writes:
            st = self.keys.get(k)
            if st is not None:
                add(st[0])
                for rk, rv in st[1].items():
                    add((rk, rv))
        out = []
        for k, v in deps.items():
            if isinstance(k, tuple) and k[0] == eng and (eng == "pe" or not selfsync) and (
                    eng in NO_SELF_SYNC or not SAME_ENGINE_SYNC):
                continue
            if self.waited[eng].get(k, 0) >= v:
                continue
            self.waited[eng][k] = v
            out.append((k, v))
        return out

    def _commit(self, me, reads, writes):
        for k in reads:
            st = self.keys.setdefault(k, [None, {}])
            if st[1].get(me[0], 0) < me[1]:
                st[1][me[0]] = me[1]
        for k in writes:
            self.keys[k] = [me, {}]

    def op(self, eng, fn, reads=(), writes=(), selfsync=False):
        sk = (eng, self.epoch)
        self._sem(sk)
        waits = self._deps(eng, reads, writes, selfsync)
        self.cnt[sk] += 1
        me = (sk, self.cnt[sk])
        self._commit(me, reads, writes)
        self.ops[eng].append((waits, fn, (sk, 1)))

    def dma(self, q, out, in_, reads=(), writes=(), stream=None, final=False, **kw):
        assert stream is not None
        self._sem(stream)
        waits = self._deps(q, reads, writes)
        prev = self.cnt[stream]
        if prev > 0 and self.waited[q].get(stream, 0) < prev:
            self.waited[q][stream] = prev
            waits.append((stream, prev))
        self.cnt[stream] += 16
        me = (stream, self.cnt[stream])
        self._commit(me, reads, writes)

        def fn(e, out=out, in_=in_, kw=kw):
            return e.dma_start(out=out, in_=in_, **kw)
        self.ops[q].append((waits, fn, (stream, 16)))
        if final:
            self.final.append(me)

    def flush(self):
        nc = self.nc
        for (k, v) in self.final:
            if self.waited["sp"].get(k, 0) < v:
                self.waited["sp"][k] = v
                self.ops["sp"].append(([(k, v)], None, None))
        self.barrier()
        ops = self.ops
        self.ops = {e: [] for e in ops}
        with nc.Block() as block:
            def run(e, name):
                for waits, fn, inc in ops[name]:
                    for (k, v) in waits:
                        e.wait_ge(self.sems[k], v)
                    if fn is None:
                        continue
                    ins = fn(e)
                    ins.then_inc(self.sems[inc[0]], inc[1])

            @block.tensor
            def _(e):
                run(e, "pe")

            @block.scalar
            def _(e):
                run(e, "act")

            @block.vector
            def _(e):
                run(e, "dve")

            @block.gpsimd
            def _(e):
                run(e, "pool")

            @block.sync
            def _(e):
                run(e, "sp")

    def emit(self):
        self.flush()
        self.stack.close()
        self.semstack.close()


def load_weight_bf16(p, wt, wkey, w_dram, kch, n, stream):
    src = w_dram.rearrange("(c p) n -> p c n", p=128)
    step = 2048
    for n0 in range(0, n, step):
        n1 = min(n, n0 + step)
        p.dma("pool", wt[:, :, n0:n1], src[:, :, n0:n1], reads=(), writes=(wkey,),
              stream=stream)


class FFNBufs:
    def __init__(self, p, T=512):
        self.T = T
        self.w_in = p.sbuf("ffn_w_in", [128, NCH, 2 * DFF], BF16)
        self.w_out = p.sbuf("ffn_w_out", [128, NFF, D], BF16)
        self.xt = [p.sbuf(f"ffn_xt{i}", [128, NCH, T], F32) for i in range(2)]
        self.sq = [p.sbuf(f"ffn_sq{i}", [128, T], F32) for i in range(2)]
        self.rstd = p.sbuf("ffn_rstd", [128, T], F32)
        self.tmp = self.sq
        self.h = p.sbuf("ffn_h", [128, NCH, T], BF16)
        self.a = p.sbuf("ffn_a", [128, NFF, T], BF16)
        self.sg = self.sq
        self.ps_ss = p.psum("ffn_ps_ss", [128, T])
        self.ps_g = [p.psum(f"ffn_ps_g{i}", [128, T]) for i in range(2)]
        self.ps_u = [p.psum(f"ffn_ps_u{i}", [128, T]) for i in range(2)]
        self.ps_o = [p.psum(f"ffn_ps_o{i}", [128, T]) for i in range(2)]


def rmsnorm_mod(p, pre, xt, xkey, T, sq, rstd, tmp, ps_ss, ones, h, hkey, gs, sh, gskey):
    for c in range(NCH):
        s = sq[c % 2]
        p.op("act", lambda e, s=s, c=c: e.activation(out=s[:, :T], in_=xt[:, c, :T], func=AF.Square),
             reads=(xkey,), writes=(pre + f"sq{c % 2}",))
        p.op("pe", lambda e, s=s, c=c: e.matmul(ps_ss[:, :T], ones[:, :], s[:, :T],
                                                start=(c == 0), stop=(c == NCH - 1)),
             reads=(pre + f"sq{c % 2}", "ones"), writes=(pre + "ps_ss",))
    p.op("act", lambda e: e.activation(out=rstd[:, :T], in_=ps_ss[:, :T], func=AF.Sqrt,
                                       bias=p.eps_ap, scale=1.0 / D),
         reads=(pre + "ps_ss", "eps"), writes=(pre + "rstd",))
    p.op("dve", lambda e: e.reciprocal(out=rstd[:, :T], in_=rstd[:, :T]),
         reads=(pre + "rstd",), writes=(pre + "rstd",))
    for c in range(NCH):
        t = tmp[c % 2]
        p.op("dve", lambda e, t=t, c=c: e.scalar_tensor_tensor(
            out=t[:, :T], in0=xt[:, c, :T], scalar=gs[:, c:c + 1], in1=rstd[:, :T],
            op0=ALU.mult, op1=ALU.mult),
            reads=(xkey, "vecs", "vecs_der", pre + "rstd"), writes=(pre + f"sq{c % 2}",))
        p.op("act", lambda e, t=t, c=c: e.activation(out=h[:, c, :T], in_=t[:, :T], func=AF.Identity,
                                                     bias=sh[:, c:c + 1], scale=1.0),
             reads=(pre + f"sq{c % 2}", "vecs", "vecs_der"), writes=(hkey,))


def ffn_tile(p, fb, xt, xkey, T, gs, sh, g2, modkey):
    rmsnorm_mod(p, "ffn_", xt, xkey, T, fb.sq, fb.rstd, fb.tmp, fb.ps_ss, p.ones, fb.h, "ffn_h",
                gs, sh, modkey)
    for j in range(NFF):
        pg, pu, sg = fb.ps_g[j % 2], fb.ps_u[j % 2], fb.sg[j % 2]
        for k in range(NCH):
            p.op("pe", lambda e, pg=pg, j=j, k=k: e.matmul(
                pg[:, :T], fb.w_in[:, k, j * 128:(j + 1) * 128], fb.h[:, k, :T],
                start=(k == 0), stop=(k == NCH - 1)),
                reads=("ffn_w_in", "ffn_h"), writes=(f"ffn_ps_g{j % 2}",))
        for k in range(NCH):
            p.op("pe", lambda e, pu=pu, j=j, k=k: e.matmul(
                pu[:, :T], fb.w_in[:, k, DFF + j * 128:DFF + (j + 1) * 128], fb.h[:, k, :T],
                start=(k == 0), stop=(k == NCH - 1)),
                reads=("ffn_w_in", "ffn_h"), writes=(f"ffn_ps_u{j % 2}",))
        p.op("act", lambda e, pg=pg, sg=sg: e.activation(out=sg[:, :T], in_=pg[:, :T], func=AF.Silu),
             reads=(f"ffn_ps_g{j % 2}",), writes=(f"ffn_sq{j % 2}",))
        p.op("dve", lambda e, pu=pu, sg=sg, j=j: e.tensor_tensor(
            out=fb.a[:, j, :T], in0=sg[:, :T], in1=pu[:, :T], op=ALU.mult),
            reads=(f"ffn_sq{j % 2}", f"ffn_ps_u{j % 2}"), writes=(f"ffn_a{j}",))
    for n in range(NCH):
        po = fb.ps_o[n % 2]
        for j in range(NFF):
            p.op("pe", lambda e, po=po, n=n, j=j: e.matmul(
                po[:, :T], fb.w_out[:, j, n * 128:(n + 1) * 128], fb.a[:, j, :T],
                start=(j == 0), stop=(j == NFF - 1)),
                reads=("ffn_w_out", f"ffn_a{j}"), writes=(f"ffn_ps_o{n % 2}",))
        p.op("dve", lambda e, po=po, n=n: e.scalar_tensor_tensor(
            out=xt[:, n, :T], in0=po[:, :T], scalar=g2[:, n:n + 1], in1=xt[:, n, :T],
            op0=ALU.mult, op1=ALU.add),
            reads=(f"ffn_ps_o{n % 2}", xkey, "vecs", "vecs_der"), writes=(xkey,))


def setup_consts(p):
    p.ones = p.sbuf("ones", [128, 128], F32)
    p.eps_t = p.sbuf("eps", [128, 1], F32)
    p.eps_ap = p.eps_t[:, 0:1]
    p.op("dve", lambda e: e.memset(p.ones[:, :], 1.0), writes=("ones",))
    p.op("dve", lambda e: e.memset(p.eps_t[:, :], EPS), writes=("eps",))


def fm(v):
    return np.ascontiguousarray(np.asarray(v, np.float32).reshape(-1, 128).T)


class Launch:
    def __init__(self):
        self.nc = bass.Bass("TRN2", target_bir_lowering=False)
        self.p = Prog(self.nc)
        self.vec_spec = []
        self.vec_off = {}
        self.nv = 0

    def inp(self, name, shape, dt=F32):
        return self.nc.dram_tensor(name, list(shape), dt, kind="ExternalInput").ap()

    def out(self, name, shape, dt=F32):
        return self.nc.dram_tensor(name, list(shape), dt, kind="ExternalOutput").ap()

    def declare_vecs(self, spec):
        for name, n in spec:
            self.vec_off[name] = (self.nv, n)
            self.nv += n
        self.vecs_dram = self.inp("vecs", [128, self.nv])

    def preamble(self, fused=False):
        p = self.p
        setup_consts(p)
        p.one_t = p.sbuf("one_c", [128, 1], F32)
        p.zero_t = p.sbuf("zero_c", [128, 1], F32)
        p.op("dve", lambda e: e.memset(p.one_t[:, :], 1.0), writes=("onec",))
        p.op("dve", lambda e: e.memset(p.zero_t[:, :], 0.0), writes=("zeroc",))
        self.vt = p.sbuf("vecs_t", [128, self.nv], F32)
        p.dma("sp", self.vt[:, :], self.vecs_dram[:, :], writes=("vecs",), stream="ld_vecs")
        self.mt = p.sbuf("modv_t", [128, 4 * 2 * 48], F32)
        if not fused:
            self.modv_dram = self.inp("modv", [128, 4 * 2 * 48])
            p.dma("sp", self.mt[:, :], self.modv_dram[:, :], writes=("vecs",), stream="ld_modv")
        self.der = p.sbuf("der_t", [128, 16 * 8], F32)
        self.nder = 0

    def V(self, name, i0=0, n=None):
        off, w = self.vec_off[name]
        n = w - i0 if n is None else n
        return self.vt[:, off + i0: off + i0 + n]

    def mod(self, layer, who, idx):
        b = (layer * 2 + who) * 48 + idx * 8
        return self.mt[:, b:b + 8]

    def gscale(self, layer, who, which):
        p = self.p
        o = self.der[:, self.nder * 8:(self.nder + 1) * 8]
        self.nder += 1
        sc = self.mod(layer, who, 1 if which == 0 else 4)
        ng = self.V(f"ng{layer}_{which}")
        p.op("dve", lambda e: e.scalar_tensor_tensor(out=o, in0=sc, scalar=1.0, in1=ng,
                                                     op0=ALU.add, op1=ALU.mult),
             reads=("vecs",), writes=("vecs_der",))
        return o

    def derived(self, fn):
        o = self.der[:, self.nder * 8:(self.nder + 1) * 8]
        self.nder += 1
        self.p.op("dve", lambda e: fn(e, o), reads=("vecs",), writes=("vecs_der",))
        return o


MODKEYS = ("vecs", "vecs_der")


def run_launch(L, in_maps):
    L.p.emit()
    res = run_bass_kernel_spmd(L.nc, in_maps, core_ids=list(range(NCORES)))
    return res.results


def build_mod():
    L = Launch()
    p = L.p
    cv = L.inp("cv", [128, 16])
    bm = L.inp("bm", [128, 4 * 48])
    w_mod = L.inp("w_mod", [4, D, 6 * D])
    modv = L.out("modv", [128, 4 * 2 * 48])
    cvt = p.sbuf("cvt", [128, 8, 2], F32)
    bmt = p.sbuf("bmt", [128, 4 * 48], F32)
    mo = p.sbuf("mo", [128, 4, 2, 48], F32)
    wb = [p.sbuf(f"wb{i}", [128, 8, 768], F32) for i in range(2)]
    ps = [p.psum(f"psm{i}", [128, 48, 2]) for i in range(2)]
    p.dma("sp", cvt[:, :, :], cv.rearrange("p (k w) -> p k w", w=2), writes=("cv",), stream="ld_cv")
    p.dma("sp", bmt[:, :], bm[:, :], writes=("bm",), stream="ld_bm")
    p.op("act", lambda e: e.activation(out=cvt[:, :, :], in_=cvt[:, :, :], func=AF.Silu),
         reads=("cv",), writes=("cv",))
    n = 0
    for i in range(4):
        wv = w_mod[i].rearrange("(c p) n -> p c n", p=128)
        for pc in range(8):
            b = wb[n % 2]
            p.dma("sp", b[:, :, :], wv[:, :, pc * 768:(pc + 1) * 768], writes=(f"wb{n % 2}",),
                  stream=f"ld_wb{n % 2}")
            for jj in range(6):
                j = pc * 6 + jj
                for kc in range(8):
                    p.op("pe", lambda e, b=b, jj=jj, j=j, kc=kc, i=i: e.matmul(
                        ps[i % 2][:, j, :], b[:, kc, jj * 128:(jj + 1) * 128], cvt[:, kc, :],
                        start=(kc == 0), stop=(kc == 7)),
                        reads=(f"wb{n % 2}", "cv"), writes=(f"psm{i % 2}",))
            n += 1
        for who in range(2):
            p.op("dve", lambda e, i=i, who=who: e.tensor_tensor(
                out=mo[:, i, who, :], in0=ps[i % 2][:, :, who], in1=bmt[:, i * 48:(i + 1) * 48],
                op=ALU.add), reads=(f"psm{i % 2}", "bm"), writes=("mo",))
    p.dma("sp", modv[:, :], mo[:, :, :, :].rearrange("p a b c -> p (a b c)"), reads=("mo",),
          stream="st_mo", final=True)
    return L


def ffn_stage(L, layer, w_in_d, w_out_d, tiles):
    p = L.p
    with p.scope():
        fb = FFNBufs(p)
        load_weight_bf16(p, fb.w_in, "ffn_w_in", w_in_d, NCH, 2 * DFF, "ld_w_in")
        load_weight_bf16(p, fb.w_out, "ffn_w_out", w_out_d, NFF, D, "ld_w_out")
        mods = {}
        for who in sorted(set(t[5] for t in tiles)):
            mods[who] = (L.gscale(layer, who, 1), L.mod(layer, who, 3), L.mod(layer, who, 5))
        def load(n):
            (src, s0, dst, d0, T, who) = tiles[n]
            sv = src.rearrange("(c p) t -> p c t", p=128)
            p.dma("sp", fb.xt[n % 2][:, :, :T], sv[:, :, s0:s0 + T], writes=(f"ffn_xt{n % 2}",), stream=f"ld_x{n % 2}")
        load(0)
        for n, (src, s0, dst, d0, T, who) in enumerate(tiles):
            xt = fb.xt[n % 2]
            xkey = f"ffn_xt{n % 2}"
            dv = dst.rearrange("(c p) t -> p c t", p=128)
            if n + 1 < len(tiles):
                load(n + 1)
            gs, sh, g2 = mods[who]
            ffn_tile(p, fb, xt, xkey, T, gs, sh, g2, "vecs_der")
            p.dma("sp", dv[:, :, d0:d0 + T], xt[:, :, :T], reads=(xkey,), stream=f"st_x{n % 2}",
                  final=True)


def tok_tiles(total, step):
    out = []
    t = 0
    while t < total:
        out.append((t, min(step, total - t)))
        t += step
    return out


CONV_K = 31
CONV_TO = 512 - (CONV_K - 1)


def conv_stage(L, layer, j, w1_d, w2_d, ident_d, tiles):
    p = L.p
    pre = f"a{j}_"
    with p.scope():
        w1 = p.sbuf("cv_w1", [128, NCH, 2 * D], BF16)
        w2 = p.sbuf("cv_w2", [128, NCH, D], BF16)
        load_weight_bf16(p, w1, "cv_w1", w1_d, NCH, 2 * D, "ld_cw1")
        load_weight_bf16(p, w2, "cv_w2", w2_d, NCH, D, "ld_cw2")
        idf = p.sbuf("cv_idf", [128, 128], F32)
        p.dma("sp", idf[:, :], ident_d[:, :], writes=("cv_idf",), stream="ld_idf")
        diag = p.sbuf("cv_diag", [128, NCH, CONV_K, 128], BF16)
        wdw = L.V(pre + "wdw")
        for c in range(NCH):
            for k in range(CONV_K):
                p.op("pool", lambda e, c=c, k=k: e.tensor_scalar(
                    out=diag[:, c, k, :], in0=idf[:, :], scalar1=wdw[:, k * 8 + c:k * 8 + c + 1],
                    scalar2=None, op0=ALU.mult),
                    reads=("cv_idf", "vecs"), writes=("cv_diag",))
        xts = [p.sbuf(f"cv_xt{i}", [128, NCH, 512], F32) for i in range(2)]
        h = p.sbuf("cv_h", [128, NCH, 512], BF16)
        u = p.sbuf("cv_u", [128, NCH, 512], BF16)
        v = p.sbuf("cv_v", [128, NCH, 512], F32)
        z = p.sbuf("cv_z", [128, NCH, 512], BF16)
        sq = [p.sbuf(f"cv_sq{i}", [128, 512], F32) for i in range(2)]
        rstd = p.sbuf("cv_rstd", [128, 512], F32)
        mean = p.sbuf("cv_mean", [128, 512], F32)
        lrs = p.sbuf("cv_lrs", [128, 512], F32)
        ps_ss = p.psum("cv_ps_ss", [128, 512])
        ps_a2 = [p.psum(f"cv_ps_a{i}", [128, 512]) for i in range(2)]
        ps_b2 = [p.psum(f"cv_ps_b{i}", [128, 512]) for i in range(2)]
        ps_c2 = [p.psum(f"cv_ps_c{i}", [128, 512]) for i in range(2)]
        ps_m = ps_ss
        ps_q = p.psum("cv_ps_q", [128, 512])
        b1, bdw, lng, lnb, b2 = (L.V(pre + n) for n in ("b1", "bdw", "lng", "lnb", "b2"))
        mods = {}
        for who in sorted(set(t[5] for t in tiles)):
            mods[who] = (L.gscale(layer, who, 0), L.mod(layer, who, 0), L.mod(layer, who, 2))
        RD = ("vecs", "vecs_der")
        def load(n):
            (src, s0, dst, d0, To, who, mL, mR) = tiles[n]
            sv = src.rearrange("(c p) t -> p c t", p=128)
            p.dma("sp", xts[n % 2][:, :, :To + CONV_K - 1], sv[:, :, s0:s0 + To + CONV_K - 1], writes=(f"cv_xt{n % 2}",),
                  stream=f"ld_cvx{n % 2}")

        def body(n, src, s0, dst, d0, To, who, mL, mR):
            Tu = To + CONV_K - 1
            xt = xts[n % 2]
            XK = f"cv_xt{n % 2}"
            if n + 1 < len(tiles):
                load(n + 1)
            gs, sh, g1 = mods[who]
            sv = src.rearrange("(c p) t -> p c t", p=128)
            dv = dst.rearrange("(c p) t -> p c t", p=128)
            rmsnorm_mod(p, "cv_", xt, XK, Tu, sq, rstd, sq, ps_ss, p.ones, h, "cv_h", gs, sh, None)
            for c in range(NCH):
                ps_a, ps_b = ps_a2[c % 2], ps_b2[c % 2]
                ka, kb_ = f"cv_ps_a{c % 2}", f"cv_ps_b{c % 2}"
                for k in range(NCH):
                    p.op("pe", lambda e, c=c, k=k, ps_a=ps_a: e.matmul(
                        ps_a[:, :Tu], w1[:, k, c * 128:(c + 1) * 128], h[:, k, :Tu],
                        start=(k == 0), stop=(k == NCH - 1)),
                        reads=("cv_w1", "cv_h"), writes=(ka,))
                for k in range(NCH):
                    p.op("pe", lambda e, c=c, k=k, ps_b=ps_b: e.matmul(
                        ps_b[:, :Tu], w1[:, k, D + c * 128:D + (c + 1) * 128], h[:, k, :Tu],
                        start=(k == 0), stop=(k == NCH - 1)),
                        reads=("cv_w1", "cv_h"), writes=(kb_,))
                s = sq[c % 2]
                p.op("act", lambda e, s=s, c=c, ps_b=ps_b: e.activation(
                    out=s[:, :Tu], in_=ps_b[:, :Tu], func=AF.Sigmoid, bias=b1[:, 8 + c:9 + c], scale=1.0),
                    reads=(kb_,) + RD, writes=(f"cv_sq{c % 2}",))
                p.op("dve", lambda e, s=s, c=c, ps_a=ps_a: e.scalar_tensor_tensor(
                    out=u[:, c, :Tu], in0=ps_a[:, :Tu], scalar=b1[:, c:c + 1], in1=s[:, :Tu],
                    op0=ALU.add, op1=ALU.mult),
                    reads=(ka, f"cv_sq{c % 2}") + RD, writes=(f"cv_u{c}",))
                if mL is not None:
                    p.op("dve", lambda e, c=c, mL=mL: e.tensor_scalar(
                        out=u[:, c, 0:15], in0=u[:, c, 0:15], scalar1=mL, scalar2=None, op0=ALU.mult),
                        reads=(f"cv_u{c}",) + RD, writes=(f"cv_u{c}",))
                if mR is not None:
                    p.op("dve", lambda e, c=c, mR=mR: e.tensor_scalar(
                        out=u[:, c, Tu - 15:Tu], in0=u[:, c, Tu - 15:Tu], scalar1=mR, scalar2=None,
                        op0=ALU.mult),
                        reads=(f"cv_u{c}",) + RD, writes=(f"cv_u{c}",))
            for c in range(NCH):
                ps_c = ps_c2[c % 2]
                kc_ = f"cv_ps_c{c % 2}"
                for k in range(CONV_K):
                    p.op("pe", lambda e, c=c, k=k, ps_c=ps_c: e.matmul(
                        ps_c[:, :To], diag[:, c, k, :], u[:, c, k:k + To],
                        start=(k == 0), stop=(k == CONV_K - 1)),
                        reads=("cv_diag", f"cv_u{c}"), writes=(kc_,))
                s = sq[c % 2]
                p.op("act", lambda e, c=c, ps_c=ps_c: e.activation(
                    out=v[:, c, :To], in_=ps_c[:, :To], func=AF.Identity, bias=bdw[:, c:c + 1], scale=1.0),
                    reads=(kc_,) + RD, writes=(f"cv_v{c}",))
                p.op("act", lambda e, c=c, s=s, ps_c=ps_c: e.activation(
                    out=s[:, :To], in_=ps_c[:, :To], func=AF.Square, bias=bdw[:, c:c + 1], scale=1.0),
                    reads=(kc_,) + RD, writes=(f"cv_sq{c % 2}",))
                p.op("pe", lambda e, c=c: e.matmul(ps_m[:, :To], p.ones[:, :], v[:, c, :To],
                                                   start=(c == 0), stop=(c == NCH - 1)),
                     reads=(f"cv_v{c}", "ones"), writes=("cv_ps_ss",))
                p.op("pe", lambda e, c=c, s=s: e.matmul(ps_q[:, :To], p.ones[:, :], s[:, :To],
                                                        start=(c == 0), stop=(c == NCH - 1)),
                     reads=(f"cv_sq{c % 2}", "ones"), writes=("cv_ps_q",))
            p.op("act", lambda e: e.activation(out=mean[:, :To], in_=ps_m[:, :To], func=AF.Identity,
                                               bias=p.zero_t[:, 0:1], scale=1.0 / D),
                 reads=("cv_ps_ss", "zeroc"), writes=("cv_mean",))
            p.op("dve", lambda e: e.tensor_tensor(out=lrs[:, :To], in0=mean[:, :To], in1=mean[:, :To],
                                                  op=ALU.mult),
                 reads=("cv_mean",), writes=("cv_lrs",))
            p.op("dve", lambda e: e.scalar_tensor_tensor(
                out=lrs[:, :To], in0=ps_q[:, :To], scalar=1.0 / D, in1=lrs[:, :To],
                op0=ALU.mult, op1=ALU.subtract),
                reads=("cv_ps_q", "cv_lrs"), writes=("cv_lrs",))
            p.op("act", lambda e: e.activation(out=lrs[:, :To], in_=lrs[:, :To], func=AF.Sqrt,
                                               bias=p.eps_ap, scale=1.0),
                 reads=("cv_lrs", "eps"), writes=("cv_lrs",))
            p.op("dve", lambda e: e.reciprocal(out=lrs[:, :To], in_=lrs[:, :To]),
                 reads=("cv_lrs",), writes=("cv_lrs",))
            for c in range(NCH):
                s = sq[c % 2]
                p.op("dve", lambda e, c=c, s=s: e.tensor_tensor(
                    out=s[:, :To], in0=v[:, c, :To], in1=mean[:, :To], op=ALU.subtract),
                    reads=(f"cv_v{c}", "cv_mean"), writes=(f"cv_sq{c % 2}",))
                p.op("dve", lambda e, c=c, s=s: e.scalar_tensor_tensor(
                    out=s[:, :To], in0=s[:, :To], scalar=lng[:, c:c + 1], in1=lrs[:, :To],
                    op0=ALU.mult, op1=ALU.mult),
                    reads=(f"cv_sq{c % 2}", "cv_lrs") + RD, writes=(f"cv_sq{c % 2}",))
                p.op("act", lambda e, c=c, s=s: e.activation(
                    out=z[:, c, :To], in_=s[:, :To], func=AF.Silu, bias=lnb[:, c:c + 1], scale=1.0),
                    reads=(f"cv_sq{c % 2}",) + RD, writes=(f"cv_z{c}",))
            for n in range(NCH):
                ps_o = ps_a2[n % 2]
                ko = f"cv_ps_a{n % 2}"
                for k in range(NCH):
                    p.op("pe", lambda e, n=n, k=k, ps_o=ps_o: e.matmul(
                        ps_o[:, :To], w2[:, k, n * 128:(n + 1) * 128], z[:, k, :To],
                        start=(k == 0), stop=(k == NCH - 1)),
                        reads=("cv_w2", f"cv_z{k}"), writes=(ko,))
                s = sq[n % 2]
                p.op("act", lambda e, n=n, s=s, ps_o=ps_o: e.activation(
                    out=s[:, :To], in_=ps_o[:, :To], func=AF.Identity, bias=b2[:, n:n + 1], scale=1.0),
                    reads=(ko,) + RD, writes=(f"cv_sq{n % 2}",))
                p.op("dve", lambda e, n=n, s=s: e.scalar_tensor_tensor(
                    out=v[:, n, :To], in0=s[:, :To], scalar=g1[:, n:n + 1], in1=xt[:, n, 15:15 + To],
                    op0=ALU.mult, op1=ALU.add),
                    reads=(f"cv_sq{n % 2}", XK) + RD, writes=(f"cv_v{n}",))
            p.dma("sp", dv[:, :, d0:d0 + To], v[:, :, :To], reads=tuple(f"cv_v{c}" for c in range(NCH)),
                  stream="st_cv", final=True)
        load(0)
        for n_, t_ in enumerate(tiles):
            body(n_, *t_)


LRU_TO = 508


def lruA_stage(L, layer, w_in_d, w_rg_d, w_ig_d, tiles, outs, ea_d, ntt):
    p = L.p
    RD = ("vecs", "vecs_der")
    with p.scope():
        win = p.sbuf("lr_win", [128, NCH, 2 * D], BF16)
        load_weight_bf16(p, win, "lr_win", w_in_d, NCH, 2 * D, "ld_lwin")
        wrg = p.sbuf("lr_wrg", [128, 16, 128], BF16)
        wig = p.sbuf("lr_wig", [128, 16, 128], BF16)
        p.dma("pool", wrg[:, :, :], w_rg_d.rearrange("d n i j -> i (d n) j"), writes=("lr_wrg",), stream="ld_wrg")
        p.dma("pool", wig[:, :, :], w_ig_d.rearrange("d n i j -> i (d n) j"), writes=("lr_wig",), stream="ld_wig")
        zeros = p.sbuf("lr_zeros", [128, 512], F32)
        p.op("pool", lambda e: e.memset(zeros[:, :], 0.0), writes=("lr_zeros",))
        cn = p.sbuf("lr_cn", [128, 48], F32)
        lam = L.V("b_lam")
        p.op("act", lambda e: e.activation(out=cn[:, 32:48], in_=lam, func=AF.Exp, bias=p.zero_t[:, 0:1], scale=-1.0),
             reads=RD + ("zeroc",), writes=("lr_cn",))
        p.op("act", lambda e: e.activation(out=cn[:, 32:48], in_=cn[:, 32:48], func=AF.Ln, bias=p.one_t[:, 0:1], scale=1.0),
             reads=("lr_cn", "onec"), writes=("lr_cn",), selfsync=True)
        p.op("dve", lambda e: e.tensor_scalar(out=cn[:, 0:16], in0=cn[:, 32:48], scalar1=-8.0, scalar2=None, op0=ALU.mult),
             reads=("lr_cn",), writes=("lr_cn",))
        p.op("dve", lambda e: e.tensor_scalar(out=cn[:, 16:32], in0=cn[:, 32:48], scalar1=-16.0, scalar2=None, op0=ALU.mult),
             reads=("lr_cn",), writes=("lr_cn",))
        xt = p.sbuf("lr_xt", [128, NCH, 512], F32)
        tas = [xt[:, c_, :] for c_ in range(NCH)]
        h = p.sbuf("lr_h", [128, NCH, 512], BF16)
        gl = p.sbuf("lr_gl", [128, NCH, 512], F32)
        xr = p.sbuf("lr_xr", [128, NCH, 512], F32)
        xrb = p.sbuf("lr_xrb", [128, NCH, 512], BF16)
        st = p.sbuf("lr_s", [128, NCH, 512], F32)
        pf = p.sbuf("lr_pf", [128, NCH, 512], F32)
        pb = p.sbuf("lr_pb", [128, NCH, 512], F32)
        sq = [p.sbuf(f"lr_sq{i}", [128, 512], F32) for i in range(2)]
        rstd = p.sbuf("lr_rstd", [128, 512], F32)
        trs = [p.sbuf(f"lr_tr{i}", [128, 512], F32) for i in range(4)]
        tgs = [p.sbuf(f"lr_tg{i}", [128, 512], F32) for i in range(4)]
        tbs = [p.sbuf(f"lr_tb{i}", [128, 512], F32) for i in range(8)]
        ths = [p.sbuf(f"lr_th{i}", [128, 512], F32) for i in range(2)]
        ea = p.sbuf("lr_ea", [128, 4, 8, ntt], F32)
        ps_ss = p.psum("lr_ps_ss", [128, 512])
        ps_a = p.psum("lr_ps_a", [128, 512])
        ps_b = p.psum("lr_ps_b", [128, 512])
        ps_rs = [p.psum(f"lr_ps_r{i}", [128, 512]) for i in range(2)]
        ps_is = [p.psum(f"lr_ps_i{i}", [128, 512]) for i in range(2)]
        b_in, wc, bcv, brg, big = (L.V(n) for n in ("b_bin", "b_wc", "b_bcv", "b_brg", "b_big"))
        mods = {}
        for who in sorted(set(t[3] for t in tiles)):
            mods[who] = (L.gscale(layer, who, 0), L.mod(layer, who, 0))
        def body(src, s0, To, who, mL, mR, d0, kidx):
            Tu = To + 4
            gs, sh = mods[who]
            sv = src.rearrange("(c p) t -> p c t", p=128)
            p.dma("sp", xt[:, :, :Tu], sv[:, :, s0:s0 + Tu],
                  writes=("lr_xt",) + tuple(f"lr_ta{c_}" for c_ in range(NCH)), stream="ld_lrx")
            rmsnorm_mod(p, "lr_", xt, "lr_xt", Tu, sq, rstd, sq, ps_ss, p.ones, h, "lr_h", gs, sh, None)
            for c in range(NCH):
                pq = (ps_a, ps_b)[c % 2]
                pk = ("lr_ps_a", "lr_ps_b")[c % 2]
                for k in range(NCH):
                    p.op("pe", lambda e, c=c, k=k, pq=pq: e.matmul(
                        pq[:, :To], win[:, k, c * 128:(c + 1) * 128], h[:, k, 2:2 + To],
                        start=(k == 0), stop=(k == NCH - 1)),
                        reads=("lr_win", "lr_h"), writes=(pk,))
                p.op("act", lambda e, c=c, pq=pq: e.activation(
                    out=gl[:, c, :To], in_=pq[:, :To], func=AF.Gelu, bias=b_in[:, c:c + 1], scale=1.0),
                    reads=(pk,) + RD, writes=("lr_gl",))
            for c in range(NCH):
                pq = (ps_a, ps_b)[c % 2]
                pk = ("lr_ps_a", "lr_ps_b")[c % 2]
                for k in range(NCH):
                    p.op("pe", lambda e, c=c, k=k, pq=pq: e.matmul(
                        pq[:, :Tu], win[:, k, D + c * 128:D + (c + 1) * 128], h[:, k, :Tu],
                        start=(k == 0), stop=(k == NCH - 1)),
                        reads=("lr_win", "lr_h"), writes=(pk,))
                s = sq[c % 2]
                sk = f"lr_sq{c % 2}"
                p.op("act", lambda e, c=c, s=s, pq=pq: e.activation(
                    out=s[:, :Tu], in_=pq[:, :Tu], func=AF.Identity, bias=b_in[:, 8 + c:9 + c], scale=1.0),
                    reads=(pk,) + RD, writes=(sk,))
                if mL is not None:
                    p.op("dve", lambda e, s=s, mL=mL: e.tensor_scalar(
                        out=s[:, 0:2], in0=s[:, 0:2], scalar1=mL, scalar2=None, op0=ALU.mult),
                        reads=(sk,) + RD, writes=(sk,))
                if mR is not None:
                    p.op("dve", lambda e, s=s, mR=mR: e.tensor_scalar(
                        out=s[:, Tu - 2:Tu], in0=s[:, Tu - 2:Tu], scalar1=mR, scalar2=None, op0=ALU.mult),
                        reads=(sk,) + RD, writes=(sk,))
                p.op("dve", lambda e, c=c, s=s: e.tensor_scalar(
                    out=xr[:, c, :To], in0=s[:, 0:To], scalar1=wc[:, c:c + 1], scalar2=bcv[:, c:c + 1],
                    op0=ALU.mult, op1=ALU.add),
                    reads=(sk,) + RD, writes=(f"lr_xr{c}",))
                for k in range(1, 5):
                    p.op("dve", lambda e, c=c, s=s, k=k: e.scalar_tensor_tensor(
                        out=xr[:, c, :To], in0=s[:, k:k + To], scalar=wc[:, k * 8 + c:k * 8 + c + 1],
                        in1=xr[:, c, :To], op0=ALU.mult, op1=ALU.add),
                        reads=(sk, f"lr_xr{c}") + RD, writes=(f"lr_xr{c}",))
                p.op("pool", lambda e, c=c: e.tensor_copy(out=xrb[:, c, :To], in_=xr[:, c, :To]),
                     reads=(f"lr_xr{c}",), writes=(f"lr_xrb{c}",))
            for d in range(2):
                for g in range(2):
                    cs_ = list(range(4 * g, 4 * g + 4))
                    for c in cs_:
                        i = d * 8 + c
                        par = c % 2
                        ps_r, ps_i = ps_rs[par], ps_is[par]
                        KPR, KPI = f"lr_ps_r{par}", f"lr_ps_i{par}"
                        tr, tg = trs[c % 4], tgs[c % 4]
                        p.op("pe", lambda e, c=c, i=i, ps_r=ps_r: e.matmul(ps_r[:, :To], wrg[:, i, :], xrb[:, c, :To],
                                                                           start=True, stop=True),
                             reads=("lr_wrg", f"lr_xrb{c}"), writes=(KPR,))
                        p.op("pe", lambda e, c=c, i=i, ps_i=ps_i: e.matmul(ps_i[:, :To], wig[:, i, :], xrb[:, c, :To],
                                                                           start=True, stop=True),
                             reads=("lr_wig", f"lr_xrb{c}"), writes=(KPI,))
                        p.op("act", lambda e, i=i, tr=tr, ps_r=ps_r: e.activation(
                            out=tr[:, :To], in_=ps_r[:, :To], func=AF.Sigmoid, bias=brg[:, i:i + 1], scale=1.0),
                            reads=(KPR,) + RD, writes=(f"lr_tr{c % 4}",))
                        p.op("act", lambda e, i=i, tg=tg, ps_i=ps_i: e.activation(
                            out=tg[:, :To], in_=ps_i[:, :To], func=AF.Sigmoid, bias=big[:, i:i + 1], scale=1.0),
                            reads=(KPI,) + RD, writes=(f"lr_tg{c % 4}",))
                    for c in cs_:
                        i = d * 8 + c
                        tr, ta, tb = trs[c % 4], tas[c], tbs[c]
                        p.op("act", lambda e, i=i, tr=tr, ta=ta: e.activation(
                            out=ta[:, :To], in_=tr[:, :To], func=AF.Exp, bias=p.zero_t[:, 0:1], scale=cn[:, i:i + 1]),
                            reads=(f"lr_tr{c % 4}", "lr_cn", "zeroc"), writes=(f"lr_ta{c}",))
                        p.op("act", lambda e, i=i, tr=tr, tb=tb: e.activation(
                            out=tb[:, :To], in_=tr[:, :To], func=AF.Exp, bias=p.zero_t[:, 0:1], scale=cn[:, 16 + i:17 + i]),
                            reads=(f"lr_tr{c % 4}", "lr_cn", "zeroc"), writes=(f"lr_tb{c}",))
                    for c in cs_:
                        tb = tbs[c]
                        p.op("act", lambda e, tb=tb: e.activation(
                            out=tb[:, :To], in_=tb[:, :To], func=AF.Sqrt, bias=p.one_t[:, 0:1], scale=-1.0),
                            reads=(f"lr_tb{c}", "onec"), writes=(f"lr_tb{c}",))
                    for c in cs_:
                        tg, ta, tb, th = tgs[c % 4], tas[c], tbs[c], ths[c % 2]
                        KA, KB, KH = f"lr_ta{c}", f"lr_tb{c}", f"lr_th{c % 2}"
                        p.op("pool", lambda e, tb=tb, tg=tg: e.tensor_tensor(
                            out=tb[:, :To], in0=tb[:, :To], in1=tg[:, :To], op=ALU.mult),
                            reads=(KB, f"lr_tg{c % 4}"), writes=(KB,))
                        p.op("dve", lambda e, c=c, tb=tb: e.tensor_tensor(
                            out=tb[:, :To], in0=tb[:, :To], in1=xr[:, c, :To], op=ALU.mult),
                            reads=(KB, f"lr_xr{c}"), writes=(KB,))
                        if d == 0:
                            p.op("dve", lambda e, c=c, ta=ta, tb=tb: e.tensor_tensor_scan(
                                out=st[:, c, :To], data0=ta[:, :To], data1=tb[:, :To], initial=0.0,
                                op0=ALU.mult, op1=ALU.add),
                                reads=(KA, KB), writes=(f"lr_s{c}",))
                            p.op("dve", lambda e, c=c, ta=ta: e.tensor_tensor_scan(
                                out=pf[:, c, :To], data0=ta[:, :To], data1=zeros[:, :To], initial=1.0,
                                op0=ALU.mult, op1=ALU.add),
                                reads=(KA, "lr_zeros"), writes=(f"lr_pf{c}",))
                            p.op("pool", lambda e, c=c: e.tensor_copy(out=ea[:, 0, c, kidx:kidx + 1], in_=st[:, c, To - 1:To]),
                                 reads=(f"lr_s{c}",), writes=("lr_ea",))
                            p.op("pool", lambda e, c=c: e.tensor_copy(out=ea[:, 1, c, kidx:kidx + 1], in_=pf[:, c, To - 1:To]),
                                 reads=(f"lr_pf{c}",), writes=("lr_ea",))
                        else:
                            p.op("dve", lambda e, c=c, ta=ta, tb=tb, th=th: e.tensor_tensor_scan(
                                out=th[:, 0:To][:, ::-1], data0=ta[:, 0:To][:, ::-1],
                                data1=tb[:, 0:To][:, ::-1], initial=0.0, op0=ALU.mult, op1=ALU.add),
                                reads=(KA, KB), writes=(KH,))
                            p.op("dve", lambda e, c=c, ta=ta: e.tensor_tensor_scan(
                                out=pb[:, c, 0:To][:, ::-1], data0=ta[:, 0:To][:, ::-1], data1=zeros[:, :To],
                                initial=1.0, op0=ALU.mult, op1=ALU.add),
                                reads=(KA, "lr_zeros"), writes=(f"lr_pb{c}",))
                            p.op("pool", lambda e, c=c, th=th: e.tensor_copy(out=ea[:, 2, c, kidx:kidx + 1], in_=th[:, 0:1]),
                                 reads=(KH,), writes=("lr_ea",))
                            p.op("pool", lambda e, c=c: e.tensor_copy(out=ea[:, 3, c, kidx:kidx + 1], in_=pb[:, c, 0:1]),
                                 reads=(f"lr_pb{c}",), writes=("lr_ea",))
                            p.op("dve", lambda e, c=c, th=th: e.tensor_tensor(
                                out=st[:, c, :To], in0=st[:, c, :To], in1=th[:, :To], op=ALU.add),
                                reads=(f"lr_s{c}", KH), writes=(f"lr_s{c}",))
            for (buf, key, dd, nm) in ((st, "lr_s", outs[0], "s"), (pf, "lr_pf", outs[1], "pf"),
                                        (pb, "lr_pb", outs[2], "pb")):
                dv = dd.rearrange("(c p) t -> p c t", p=128)
                p.dma("sp", dv[:, :, d0:d0 + To], buf[:, :, :To], reads=tuple(f"{key}{c}" for c in range(NCH)),
                      stream="st_lr" + nm, final=True)
            dv = outs[3].rearrange("(c p) t -> p c t", p=128)
            p.dma("sp", dv[:, :, d0:d0 + To], gl[:, :, :To], reads=("lr_gl",), stream="st_lrgl", final=True)
        for t_ in tiles:
            body(*t_)
        p.dma("sp", ea_d[:, :], ea[:, :, :, :].rearrange("p a b c -> p (a b c)"), reads=("lr_ea",),
              stream="st_ea", final=True)


def lruB_stage(L, layer, w_out_d, tiles, ins, ea_own_d, ea_par_d, ntl, ntt, fA, fB):
    p = L.p
    RD = ("vecs", "vecs_der")
    with p.scope():
        wo = p.sbuf("lb_wo", [128, NCH, D], BF16)
        load_weight_bf16(p, wo, "lb_wo", w_out_d, NCH, D, "ld_lbwo")
        eo = p.sbuf("lb_eo", [128, 4, 8, ntt], F32)
        ep = p.sbuf("lb_ep", [128, 4, 8, ntt], F32)
        p.dma("sp", eo[:, :, :, :].rearrange("p a b c -> p (a b c)"), ea_own_d[:, :], writes=("lb_eo",), stream="ld_eo")
        ch = p.sbuf("lb_ch", [128, 8, 8], F32)
        cf = p.sbuf("lb_cf", [128, ntt + 1, 8], F32)
        cb = p.sbuf("lb_cb", [128, ntt + 1, 8], F32)
        K = ("lb_chain",)

        def step(out, E, A, st):
            p.op("dve", lambda e: e.tensor_tensor(out=ch[:, 4, :], in0=A, in1=st, op=ALU.mult),
                 reads=K + ("lb_eo", "lb_ep"), writes=K)
            p.op("dve", lambda e: e.tensor_tensor(out=out, in0=ch[:, 4, :], in1=E, op=ALU.add),
                 reads=K + ("lb_eo", "lb_ep"), writes=K)
        ctxF = eo[:, 0, :, ntl]
        ctxB = eo[:, 2, :, ntl]
        p.op("dve", lambda e: e.tensor_copy(out=cf[:, 0, :], in_=ctxF), reads=("lb_eo",), writes=K)
        p.op("dve", lambda e: e.tensor_copy(out=cb[:, ntl - 1, :], in_=ctxB), reads=("lb_eo",), writes=K)
        for k in range(ntl - 1):
            step(cf[:, k + 1, :], eo[:, 0, :, k], eo[:, 1, :, k], cf[:, k, :])
        for k in range(ntl - 1, 0, -1):
            step(cb[:, k - 1, :], eo[:, 2, :, k], eo[:, 3, :, k], cb[:, k, :])
        p.op("dve", lambda e: e.memset(cf[:, ntl, :], 0.0), reads=K, writes=K)
        p.op("dve", lambda e: e.memset(cb[:, ntl, :], 0.0), reads=K, writes=K)
        st = p.sbuf("lb_s", [128, NCH, 512], F32)
        pf = p.sbuf("lb_pf", [128, NCH, 512], F32)
        pb = p.sbuf("lb_pb", [128, NCH, 512], F32)
        gl = p.sbuf("lb_gl", [128, NCH, 512], F32)
        xt = p.sbuf("lb_xt", [128, NCH, 512], F32)
        y = p.sbuf("lb_y", [128, NCH, 512], BF16)
        sq = [p.sbuf(f"lb_sq{i}", [128, 512], F32) for i in range(2)]
        ps_o = [p.psum(f"lb_ps_o{i}", [128, 512]) for i in range(2)]
        bo = L.V("b_bo")
        g1s = {who: L.mod(layer, who, 2) for who in sorted(set(t[1] for t in tiles))}
        def body(To, who, d0, kidx, xsrc, xs0, xdst, xd0):
            for (buf, key, dd, nm) in ((st, "lb_s", ins[0], "s"), (pf, "lb_pf", ins[1], "pf"),
                                        (pb, "lb_pb", ins[2], "pb"), (gl, "lb_gl", ins[3], "gl")):
                dv = dd.rearrange("(c p) t -> p c t", p=128)
                p.dma("sp", buf[:, :, :To], dv[:, :, d0:d0 + To], writes=(key,), stream="ld_lb" + nm)
            xv = xsrc.rearrange("(c p) t -> p c t", p=128)
            p.dma("sp", xt[:, :, :To], xv[:, :, xs0:xs0 + To], writes=("lb_xt",), stream="ld_lbx")
            for c in range(NCH):
                s = sq[c % 2]
                sk = f"lb_sq{c % 2}"
                p.op("dve", lambda e, c=c, s=s: e.scalar_tensor_tensor(
                    out=s[:, :To], in0=pf[:, c, :To], scalar=cf[:, kidx, c:c + 1], in1=st[:, c, :To],
                    op0=ALU.mult, op1=ALU.add), reads=("lb_pf", "lb_s") + K, writes=(sk,))
                p.op("dve", lambda e, c=c, s=s: e.scalar_tensor_tensor(
                    out=s[:, :To], in0=pb[:, c, :To], scalar=cb[:, kidx, c:c + 1], in1=s[:, :To],
                    op0=ALU.mult, op1=ALU.add), reads=("lb_pb", sk) + K, writes=(sk,))
                p.op("dve", lambda e, c=c, s=s: e.tensor_tensor(
                    out=y[:, c, :To], in0=s[:, :To], in1=gl[:, c, :To], op=ALU.mult),
                    reads=(sk, "lb_gl"), writes=(f"lb_y{c}",))
            for n in range(NCH):
                po = ps_o[n % 2]
                for k in range(NCH):
                    p.op("pe", lambda e, n=n, k=k, po=po: e.matmul(
                        po[:, :To], wo[:, k, n * 128:(n + 1) * 128], y[:, k, :To],
                        start=(k == 0), stop=(k == NCH - 1)),
                        reads=("lb_wo", f"lb_y{k}"), writes=(f"lb_ps_o{n % 2}",))
                s = sq[n % 2]
                sk = f"lb_sq{n % 2}"
                p.op("act", lambda e, n=n, s=s, po=po: e.activation(
                    out=s[:, :To], in_=po[:, :To], func=AF.Identity, bias=bo[:, n:n + 1], scale=1.0),
                    reads=(f"lb_ps_o{n % 2}",) + RD, writes=(sk,))
                g1 = g1s[who]
                p.op("dve", lambda e, n=n, s=s, g1=g1: e.scalar_tensor_tensor(
                    out=xt[:, n, :To], in0=s[:, :To], scalar=g1[:, n:n + 1], in1=xt[:, n, :To],
                    op0=ALU.mult, op1=ALU.add), reads=(sk, "lb_xt") + RD, writes=("lb_xt",))
            dv = xdst.rearrange("(c p) t -> p c t", p=128)
            p.dma("sp", dv[:, :, xd0:xd0 + To], xt[:, :, :To], reads=("lb_xt",), stream="st_lbx", final=True)
        for t_ in tiles:
            body(*t_)


def qkv_stage(L, layer, w_qkv_d, tiles, q_d, k_d, v_d, bvrow_d):
    p = L.p
    RD = ("vecs", "vecs_der")
    with p.scope():
        w = p.sbuf("qk_w", [128, NCH, 3 * D], BF16)
        load_weight_bf16(p, w, "qk_w", w_qkv_d, NCH, 3 * D, "ld_qkw")
        xt = p.sbuf("qk_xt", [128, NCH, 512], F32)
        h = p.sbuf("qk_h", [128, NCH, 512], BF16)
        o = [p.sbuf(f"qk_o{i}", [128, NCH, 512], BF16) for i in range(2)]
        vt = p.sbuf("qk_vt", [128, D], BF16)
        bvr = p.sbuf("qk_bvr", [128, D], F32)
        p.dma("sp", bvr[:, :], bvrow_d[:, :], writes=("qk_bvr",), stream="ld_bvr")
        sq = [p.sbuf(f"qk_sq{i}", [128, 512], F32) for i in range(2)]
        rstd = p.sbuf("qk_rstd", [128, 512], F32)
        ps_ss = p.psum("qk_ps_ss", [128, 512])
        ps = [p.psum(f"qk_ps{i}", [128, 512]) for i in range(2)]
        bq = L.V("c_bqkv")
        bq8 = L.derived(lambda e, o_: e.tensor_scalar(out=o_, in0=bq[:, 0:8], scalar1=0.125, scalar2=None, op0=ALU.mult))
        mods = {who: (L.gscale(layer, who, 0), L.mod(layer, who, 0)) for who in sorted(set(t[3] for t in tiles))}
        def body(src, s0, T, who, d0):
            gs, sh = mods[who]
            sv = src.rearrange("(c p) t -> p c t", p=128)
            p.dma("sp", xt[:, :, :T], sv[:, :, s0:s0 + T], writes=("qk_xt",), stream="ld_qkx")
            rmsnorm_mod(p, "qk_", xt, "qk_xt", T, sq, rstd, sq, ps_ss, p.ones, h, "qk_h", gs, sh, None)
            for m in range(2 * NCH):
                t3, c = divmod(m, NCH)
                pp = ps[m % 2]
                for k in range(NCH):
                    p.op("pe", lambda e, m=m, k=k, pp=pp: e.matmul(
                        pp[:, :T], w[:, k, m * 128:(m + 1) * 128], h[:, k, :T],
                        start=(k == 0), stop=(k == NCH - 1)),
                        reads=("qk_w", "qk_h"), writes=(f"qk_ps{m % 2}",))
                if t3 == 0:
                    p.op("act", lambda e, c=c, pp=pp: e.activation(
                        out=o[0][:, c, :T], in_=pp[:, :T], func=AF.Identity, bias=bq8[:, c:c + 1], scale=0.125),
                        reads=(f"qk_ps{m % 2}",) + RD, writes=("qk_o0",))
                else:
                    p.op("act", lambda e, c=c, pp=pp, t3=t3, m=m: e.activation(
                        out=o[t3][:, c, :T], in_=pp[:, :T], func=AF.Identity, bias=bq[:, m:m + 1], scale=1.0),
                        reads=(f"qk_ps{m % 2}",) + RD, writes=(f"qk_o{t3}",))
            for t3, dd in enumerate((q_d, k_d)):
                dv = dd.rearrange("(c p) t -> p c t", p=128)
                p.dma("sp", dv[:, :, d0:d0 + T], o[t3][:, :, :T], reads=(f"qk_o{t3}",), stream=f"st_qk{t3}", final=True)
            for tb in range((T + 127) // 128):
                m_ = min(128, T - tb * 128)
                for cg in range(2):
                    pp = ps[cg]
                    for k in range(NCH):
                        p.op("pe", lambda e, tb=tb, m_=m_, cg=cg, k=k, pp=pp: e.matmul(
                            pp[:m_, :512], h[:, k, tb * 128:tb * 128 + m_],
                            w[:, k, 2 * D + cg * 512:2 * D + (cg + 1) * 512],
                            start=(k == 0), stop=(k == NCH - 1)),
                            reads=("qk_w", "qk_h"), writes=(f"qk_ps{cg}",))
                    p.op("dve", lambda e, m_=m_, cg=cg, pp=pp: e.tensor_tensor(
                        out=vt[:m_, cg * 512:(cg + 1) * 512], in0=pp[:m_, :512],
                        in1=bvr[:m_, cg * 512:(cg + 1) * 512], op=ALU.add),
                        reads=(f"qk_ps{cg}", "qk_bvr"), writes=("qk_vt",))
                p.dma("sp", v_d[d0 + tb * 128:d0 + tb * 128 + m_, :], vt[:m_, :], reads=("qk_vt",),
                      stream="st_qkv", final=True)
        for t_ in tiles:
            body(*t_)


AX = mybir.AxisListType.X
NROWS = 64
GW = 64


def na_stage(L, layer, w_o_d, q_d, k_d, vtok_d, coff, tabi_d, tabb_d, identb_d, x_src, x_dst, nqrows):
    p = L.p
    RD = ("vecs", "vecs_der")
    with p.scope():
        wo = p.sbuf("na_wo", [128, NCH, D], BF16)
        load_weight_bf16(p, wo, "na_wo", w_o_d, NCH, D, "ld_nawo")
        idb = p.sbuf("na_idb", [128, 128], BF16)
        p.dma("pool", idb[:, :], identb_d[:, :], writes=("na_idb",), stream="ld_idb")
        tabi = p.sbuf("na_tabi", [128, 8, 576], F32)
        p.dma("sp", tabi[:, :, :], tabi_d[:, :, :], writes=("na_tabi",), stream="ld_tabi")
        tabb = p.sbuf("na_tabb", [128, 8, 576], F32)
        kcb = p.sbuf("na_kcb", [128, NCH, CTX], BF16)
        vcb = p.sbuf("na_vcb", [64, 4, D], BF16)
        p.dma("sp", kcb[:, :, :], k_d.rearrange("(c p) t -> p c t", p=128)[:, :, coff:coff + CTX], writes=("na_kcb",),
              stream="ld_kcb")
        p.dma("sp", vcb[:, :, :], vtok_d[coff:coff + CTX, :].rearrange("(r c) d -> c r d", c=64), writes=("na_vcb",),
              stream="ld_vcb")
        kb = p.sbuf("na_kb", [128, NCH, 16 * GW], BF16)
        vb = p.sbuf("na_vb", [64, 16, D], BF16)
        qb = p.sbuf("na_qb", [128, NCH, 256], BF16)
        ofm = p.sbuf("na_ofm", [128, NCH, 256], BF16)
        xt = p.sbuf("na_xt", [128, NCH, 256], F32)
        sl = [p.sbuf(f"na_sl{i}", [128, 576], F32) for i in range(2)]
        pl = [p.sbuf(f"na_pl{i}", [128, 1024], BF16) for i in range(2)]
        pT = [p.sbuf(f"na_pT{i}", [64, 16, 128], BF16) for i in range(2)]
        sm = [p.sbuf(f"na_sm{i}", [128, 8], F32) for i in range(2)]
        tmp = p.sbuf("na_tmp", [128, 256], F32)
        ps_s = [p.psum(f"na_ps_s{i}", [128, 512]) for i in range(2)]
        ps_sc = [p.psum(f"na_ps_sc{i}", [128, 512]) for i in range(2)]
        ps_t = p.psum("na_ps_t", [128, 16, 128], BF16)
        ps_o = p.psum("na_ps_o", [128, 512])
        ps_y = p.psum("na_ps_y", [128, 512])
        bo = L.V("c_bo")
        g1 = L.mod(layer, 0, 2)
        khv = k_d.rearrange("(c p) t -> p c t", p=128)
        qv = q_d.rearrange("(c p) t -> p c t", p=128)
        xv = x_src.rearrange("(c p) t -> p c t", p=128)
        dv = x_dst.rearrange("(c p) t -> p c t", p=128)
        it = 0
        blocks = [(4 * b_, 4) for b_ in range(nqrows // 4)]
        if nqrows % 4:
            blocks.append((4 * (nqrows // 4), nqrows % 4))
        for (i0, nr) in blocks:
            base = max(i0 - 4, 0)
            nrows = max(i0 + nr - 5, 0) + 9 - base
            nkr = 9
            nk = nkr * 64
            nblk = nkr + 4
            nt = nr * 64
            p.dma("sp", kb[:, :, :nrows * GW], khv[:, :, base * GW:(base + nrows) * GW], writes=("na_kb",), stream="ld_kb")
            p.dma("sp", vb[:, :nrows, :], vtok_d[base * GW:(base + nrows) * GW, :].rearrange("(r c) d -> c r d", c=64),
                  writes=("na_vb",), stream="ld_vb")
            p.dma("sp", qb[:, :, :nt], qv[:, :, i0 * 64:i0 * 64 + nt], writes=("na_qb",), stream="ld_qb")
            p.dma("sp", xt[:, :, :nt], xv[:, :, i0 * 64:i0 * 64 + nt], writes=("na_xt",), stream="ld_nax")
            def front(ii, hp, par, so, tab, tabkey):
                s_, p_, m_ = sl[par], pl[par], sm[par]
                ks, kp, km = f"na_sl{par}", f"na_pl{par}", f"na_sm{par}"
                pss, psc = ps_s[par], ps_sc[par]
                kpss, kpsc = f"na_ps_s{par}", f"na_ps_sc{par}"
                for hh in range(2):
                    pr = slice(hh * 64, (hh + 1) * 64)
                    p.op("pe", lambda e, pr=pr: e.matmul(
                        pss[pr, :512], qb[pr, hp, ii * 64:(ii + 1) * 64],
                        kb[pr, hp, so * 64:so * 64 + 512], start=True, stop=True),
                        reads=("na_qb", "na_kb"), writes=(kpss,))
                    p.op("pe", lambda e, pr=pr: e.matmul(
                        psc[pr, 0:64], qb[pr, hp, ii * 64:(ii + 1) * 64],
                        kb[pr, hp, so * 64 + 512:so * 64 + 576], start=True, stop=True),
                        reads=("na_qb", "na_kb"), writes=(kpsc,))
                    p.op("pe", lambda e, pr=pr: e.matmul(
                        psc[pr, 64:64 + CTX], qb[pr, hp, ii * 64:(ii + 1) * 64], kcb[pr, hp, :],
                        start=True, stop=True, skip_group_check=True),
                        reads=("na_qb", "na_kcb"), writes=(kpsc,))
                p.op("dve", lambda e: e.tensor_tensor(
                    out=s_[:, 0:512], in0=pss[:, :512], in1=tab[:, hp, 0:512], op=ALU.add),
                    reads=(kpss, tabkey), writes=(ks,))
                p.op("dve", lambda e: e.tensor_tensor(
                    out=s_[:, 512:576], in0=psc[:, 0:64], in1=tab[:, hp, 512:576], op=ALU.add),
                    reads=(kpsc, tabkey), writes=(ks,))
                p.op("dve", lambda e: e.reduce_max(out=m_[:, 0:1], in_=s_[:, :576], axis=AX),
                     reads=(ks,), writes=(km,))
                p.op("dve", lambda e: e.reduce_max(out=m_[:, 1:2], in_=psc[:, 64:64 + CTX], axis=AX),
                     reads=(kpsc,), writes=(km,))
                p.op("dve", lambda e: e.tensor_tensor(out=m_[:, 2:3], in0=m_[:, 0:1], in1=m_[:, 1:2], op=ALU.max),
                     reads=(km,), writes=(km,))
                p.op("dve", lambda e: e.tensor_scalar(out=m_[:, 3:4], in0=m_[:, 2:3], scalar1=-1.0, scalar2=None,
                                                      op0=ALU.mult),
                     reads=(km,), writes=(km,))
                p.op("act", lambda e: e.activation(
                    out=p_[:, :576], in_=s_[:, :576], func=AF.Exp, bias=m_[:, 3:4], scale=1.0, accum_out=m_[:, 4:5]),
                    reads=(ks, km), writes=(kp, km + "a"))
                p.op("act", lambda e: e.activation(
                    out=p_[:, 576:576 + CTX], in_=psc[:, 64:64 + CTX], func=AF.Exp, bias=m_[:, 3:4], scale=1.0,
                    accum_out=m_[:, 5:6]),
                    reads=(kpsc, km), writes=(kp, km + "a"))
                p.op("dve", lambda e: e.tensor_tensor(out=m_[:, 6:7], in0=m_[:, 4:5], in1=m_[:, 5:6], op=ALU.add),
                     reads=(km + "a",), writes=(km + "b",))
                p.op("dve", lambda e: e.reciprocal(out=m_[:, 7:8], in_=m_[:, 6:7]),
                     reads=(km + "b",), writes=(km + "b",))
                p.op("dve", lambda e: e.tensor_scalar(
                    out=p_[:, :576 + CTX], in0=p_[:, :576 + CTX], scalar1=m_[:, 7:8], scalar2=None, op0=ALU.mult),
                    reads=(kp, km + "b"), writes=(kp,))

            def back(ii, hp, par, so):
                p_, t_ = pl[par], pT[par]
                kp, kt = f"na_pl{par}", f"na_pT{par}"
                for j in range(nblk):
                    p.op("pe", lambda e, j=j: e.transpose(
                        out=ps_t[:64, j, :], in_=p_[:, j * 64:(j + 1) * 64], identity=idb[:, :]),
                        reads=(kp, "na_idb"), writes=("na_ps_t",))
                p.op("act", lambda e: e.activation(
                    out=t_[:, :nblk, :], in_=ps_t[:64, :nblk, :], func=AF.Identity, bias=p.zero_t[:64, 0:1], scale=1.0),
                    reads=("na_ps_t", "zeroc"), writes=(kt,))

            def back2(ii, hp, par, so):
                t_ = pT[par]
                kt = f"na_pT{par}"
                for hh in range(2):
                    h_ = 2 * hp + hh
                    for j in range(nblk):
                        if j < nkr:
                            lhs = vb[:, so + j, h_ * 64:(h_ + 1) * 64]
                        else:
                            lhs = vcb[:, j - nkr, h_ * 64:(h_ + 1) * 64]
                        p.op("pe", lambda e, hh=hh, j=j, lhs=lhs: e.matmul(
                            ps_o[hh * 64:(hh + 1) * 64, ii * 64:(ii + 1) * 64], lhs, t_[:, j, hh * 64:(hh + 1) * 64],
                            start=(j == 0), stop=(j == nblk - 1)),
                            reads=("na_vb", "na_vcb", kt), writes=(f"na_ps_o{ii}",))
                p.op("act", lambda e: e.activation(
                    out=ofm[:, hp, ii * 64:(ii + 1) * 64], in_=ps_o[:, ii * 64:(ii + 1) * 64], func=AF.Identity,
                    bias=p.zero_t[:, 0:1], scale=1.0),
                    reads=(f"na_ps_o{ii}", "zeroc"), writes=("na_ofm",))

            work = []
            for ii in range(nr):
                i = i0 + ii
                so = max(i - 4, 0) - base
                if i < 5:
                    tab, tabkey = tabb, "na_tabb"
                else:
                    tab, tabkey = tabi, "na_tabi"
                for hp in range(8):
                    work.append((ii, hp, it % 2, so, tab, tabkey, i if (i < 5 and hp == 0) else None))
                    it += 1
            nw = len(work)
            for n_ in range(nw + 2):
                if n_ < nw:
                    wk = work[n_]
                    if wk[6] is not None:
                        p.dma("sp", tabb[:, :, :], tabb_d[wk[6]], writes=("na_tabb",), stream="ld_tabb")
                    front(*wk[:6])
                if 1 <= n_ <= nw:
                    back(*work[n_ - 1][:4])
                if 2 <= n_:
                    back2(*work[n_ - 2][:4])
            for n in range(NCH):
                for k in range(NCH):
                    p.op("pe", lambda e, n=n, k=k, nt=nt: e.matmul(
                        ps_y[:, :nt], wo[:, k, n * 128:(n + 1) * 128], ofm[:, k, :nt],
                        start=(k == 0), stop=(k == NCH - 1)),
                        reads=("na_wo", "na_ofm"), writes=("na_ps_y",))
                p.op("act", lambda e, n=n, nt=nt: e.activation(out=tmp[:, :nt], in_=ps_y[:, :nt], func=AF.Identity,
                                                        bias=bo[:, n:n + 1], scale=1.0),
                     reads=("na_ps_y",) + RD, writes=("na_tmp",))
                p.op("dve", lambda e, n=n, nt=nt: e.scalar_tensor_tensor(
                    out=xt[:, n, :nt], in0=tmp[:, :nt], scalar=g1[:, n:n + 1], in1=xt[:, n, :nt],
                    op0=ALU.mult, op1=ALU.add), reads=("na_tmp", "na_xt") + RD, writes=("na_xt",))
            p.dma("sp", dv[:, :, i0 * 64:i0 * 64 + nt], xt[:, :, :nt], reads=("na_xt",), stream="st_nax", final=True)


def final_norm_stage(L, tiles):
    p = L.p
    with p.scope():
        xt = p.sbuf("fn_xt", [128, NCH, 512], F32)
        ho = p.sbuf("fn_h", [128, NCH, 512], F32)
        sq = [p.sbuf(f"fn_sq{i}", [128, 512], F32) for i in range(2)]
        rstd = p.sbuf("fn_rstd", [128, 512], F32)
        ps_ss = p.psum("fn_ps_ss", [128, 512])
        fg = L.V("final_g")
        zer = L.V("zeros8")
        for (src, s0, dst, d0, T) in tiles:
            sv = src.rearrange("(c p) t -> p c t", p=128)
            dv = dst.rearrange("(c p) t -> p c t", p=128)
            p.dma("sp", xt[:, :, :T], sv[:, :, s0:s0 + T], writes=("fn_xt",), stream="ld_fnx")
            rmsnorm_mod(p, "fn_", xt, "fn_xt", T, sq, rstd, sq, ps_ss, p.ones, ho, "fn_h", fg, zer, None)
            p.dma("sp", dv[:, :, d0:d0 + T], ho[:, :, :T], reads=("fn_h",), stream="st_fn", final=True)


def mod_stage(L, cv, bm, w_mod):
    p = L.p
    with p.scope():
        cvt = p.sbuf("cvt", [128, 8, 2], F32)
        bmt = p.sbuf("bmt", [128, 4 * 48], F32)
        wb = [p.sbuf(f"wb{i}", [128, 8, 768], BF16) for i in range(3)]
        cvb = p.sbuf("cvb", [128, 8, 2], BF16)
        ps = [p.psum(f"psm{i}", [128, 48, 2]) for i in range(2)]
        p.dma("sp", cvt[:, :, :], cv.rearrange("p (k w) -> p k w", w=2), writes=("cv",), stream="ld_cv")
        p.dma("sp", bmt[:, :], bm[:, :], writes=("bm",), stream="ld_bm")
        p.op("act", lambda e: e.activation(out=cvb[:, :, :], in_=cvt[:, :, :], func=AF.Silu),
             reads=("cv",), writes=("cvb",))
        n = 0
        for i in range(4):
            wv = w_mod[i].rearrange("(c p) n -> p c n", p=128)
            for pc in range(8):
                b = wb[n % 3]
                p.dma("pool", b[:, :, :], wv[:, :, pc * 768:(pc + 1) * 768], writes=(f"wb{n % 3}",),
                      stream=f"ld_wb{n % 3}")
                for jj in range(6):
                    j = pc * 6 + jj
                    for kc in range(8):
                        p.op("pe", lambda e, b=b, jj=jj, j=j, kc=kc, i=i: e.matmul(
                            ps[i % 2][:, j, :], b[:, kc, jj * 128:(jj + 1) * 128], cvb[:, kc, :],
                            start=(kc == 0), stop=(kc == 7)),
                            reads=(f"wb{n % 3}", "cvb"), writes=(f"psm{i % 2}",))
                n += 1
            for who in range(2):
                o0 = (i * 2 + who) * 48
                p.op("dve", lambda e, i=i, who=who, o0=o0: e.tensor_tensor(
                    out=L.mt[:, o0:o0 + 48], in0=ps[i % 2][:, :, who], in1=bmt[:, i * 48:(i + 1) * 48],
                    op=ALU.add), reads=(f"psm{i % 2}", "bm"), writes=("vecs",))


SF = SEQ
NTL = len(tok_tiles(SF, LRU_TO))
NTT = NTL + 1
NB1 = 9
R1 = NB1 * LRU_TO
NQR = 65
R2 = NQR * GW
W = SEQ // 2
NEG = -30000.0


def _taps(w):
    K = w.shape[0]
    return np.ascontiguousarray(np.asarray(w, np.float32).reshape(K, 8, 128).transpose(2, 0, 1).reshape(128, K * 8))


def _vec_spec():
    spec = [(f"ng{i}_{w}", 8) for i in range(4) for w in range(2)]
    spec += [("zeros8", 8), ("final_g", 8)]
    for j in range(2):
        spec += [(f"a{j}_b1", 16), (f"a{j}_wdw", 248), (f"a{j}_bdw", 8), (f"a{j}_lng", 8), (f"a{j}_lnb", 8), (f"a{j}_b2", 8)]
    spec += [("b_bin", 16), ("b_wc", 40), ("b_bcv", 8), ("b_brg", 16), ("b_big", 16), ("b_lam", 16), ("b_bo", 8),
             ("c_bqkv", 24), ("c_bo", 8)]
    return spec


def _vecs_host(I, hf):
    parts = [fm(I["norm_g"][i, w]) for i in range(4) for w in range(2)]
    parts += [np.zeros((128, 8), np.float32), fm(I["final_g"])]
    for j in range(2):
        wdw = I["a_w_dw"][j] if hf == 0 else I["a_w_dw"][j][::-1]
        parts += [fm(I["a_b_pw1"][j]), _taps(wdw), fm(I["a_b_dw"][j]), fm(I["a_ln_g"][j]),
                  fm(I["a_ln_b"][j]), fm(I["a_b_pw2"][j])]
    wc = np.asarray(I["b_w_conv"][0], np.float32)
    z = np.zeros((1, D), np.float32)
    wc5 = np.concatenate([wc, z], 0) if hf == 0 else np.concatenate([z, wc[::-1]], 0)
    dsel = slice(None) if hf == 0 else slice(None, None, -1)
    parts += [fm(I["b_b_in"][0]), _taps(wc5), fm(I["b_b_conv"][0]), fm(I["b_b_rg"][0][dsel].reshape(-1)),
              fm(I["b_b_ig"][0][dsel].reshape(-1)), fm(I["b_lam"][0][dsel].reshape(-1)), fm(I["b_b_out"][0]),
              fm(I["c_b_qkv"][0]), fm(I["c_b_o"][0])]
    return np.ascontiguousarray(np.concatenate(parts, axis=1).astype(np.float32))


def _na_tables(rpb, hf):
    rpb = np.asarray(rpb, np.float32)
    ql = np.arange(64)
    q = ql if hf == 0 else 63 - ql
    kc = ql if hf == 0 else 63 - ql
    cs = np.clip(q - 8, 0, 48)
    colok = (kc[None, :] >= cs[:, None]) & (kc[None, :] < cs[:, None] + 16)
    coff = np.clip(kc[None, :] - q[:, None] + 15, 0, 30)

    def table(i):
        t = np.full((2, 64, 8, 9, 64), NEG, np.float32)
        r = i if hf == 0 else 127 - i
        r0 = int(np.clip(r - 4, 0, 120))
        for j in range(9):
            krl = max(i - 4, 0) + j
            kr = krl if hf == 0 else 127 - krl
            if r0 <= kr < r0 + 8:
                vals = rpb[:, kr - r + 7][:, coff]
                vals = np.where(colok[None], vals, NEG)
                t[:, :, :, j, :] = vals.reshape(8, 2, 64, 64).transpose(1, 2, 0, 3)
        return t.reshape(128, 8, 9 * 64)
    tabi = table(30)
    tabb = np.stack([table(i) for i in range(5)], 0)
    return np.ascontiguousarray(tabi), np.ascontiguousarray(tabb)


def build_fused():
    L = Launch()
    L.declare_vecs(_vec_spec())
    nc = L.nc
    p = L.p

    def internal(name, shape, dt=F32):
        return nc.dram_tensor(name, list(shape), dt, kind="Internal").ap()
    cv = L.inp("cv", [128, 16]); bm = L.inp("bm", [128, 4 * 48]); w_mod = L.inp("w_mod", [4, D, 6 * D])
    xh = L.inp("xh", [D, SF + 30]); ch = L.inp("ch", [D, CTX + 30]); idd = L.inp("ident", [128, 128])
    a_w1 = L.inp("a_w1", [2, D, 2 * D]); a_w2 = L.inp("a_w2", [2, D, D])
    wfi = L.inp("wfi", [4, D, 2 * DFF]); wfo = L.inp("wfo", [4, DFF, D])
    win = L.inp("win", [D, 2 * D]); wrg = L.inp("wrg", [2, 8, 128, 128]); wig = L.inp("wig", [2, 8, 128, 128])
    wout = L.inp("wout", [D, D]); wqkv = L.inp("wqkv", [D, 3 * D]); wo = L.inp("wo", [D, D])
    bvrow = L.inp("bvrow", [128, D]); tid = L.inp("tabi", [128, 8, 576]); tbd = L.inp("tabb", [5, 128, 8, 576])
    od = L.out("o", [D, W])
    xm0 = internal("xm0", [D, SF]); cm0 = internal("cm0", [D, CTX])
    x0 = internal("x0", [D, SF + 4]); c0 = internal("c0", [D, CTX + 4])
    lr = [internal("lrd_" + n, [D, SF + CTX]) for n in ("s", "pf", "pb", "gl")]
    ea = internal("ea", [128, 4 * 8 * NTT])
    xm1 = internal("xm1", [D, R1]); x1 = internal("x1", [D, R1])
    cm1 = internal("cm1", [D, CTX]); c1 = internal("c1", [D, CTX])
    qd = internal("q", [D, R1 + CTX], BF16); kd = internal("k", [D, R1 + CTX], BF16)
    vtok = internal("vtok", [R1 + CTX, D], BF16)
    xm2 = internal("xm2", [D, R2]); x2h = internal("x2h", [D, 15 + R2])
    xm3 = internal("xm3", [D, W]); x3 = internal("x3", [D, W])
    L.preamble(fused=True)
    zc = p.zero_t[:, 0:1]
    zt = p.sbuf("zpad", [128, NCH, 16], F32)
    p.op("dve", lambda e: e.memset(zt[:, :, :], 0.0), writes=("zpad",))
    for (t, c0_, n) in ((x0, 0, 2), (x0, SF + 2, 2), (c0, 0, 2), (c0, CTX + 2, 2), (x2h, 0, 15)):
        p.dma("sp", t.rearrange("(c p) t -> p c t", p=128)[:, :, c0_:c0_ + n], zt[:, :, :n], reads=("zpad",),
              stream="st_zpad", final=True)
    mod_stage(L, cv, bm, w_mod)
    lat = tok_tiles(SF, 512)
    ct = tok_tiles(SF, CONV_TO)
    tiles = [(xh, t0, xm0, t0, To, 0, zc if i == 0 else None, zc if i == len(ct) - 1 else None)
             for i, (t0, To) in enumerate(ct)]
    tiles.append((ch, 0, cm0, 0, CTX, 1, zc, zc))
    conv_stage(L, 0, 0, a_w1[0], a_w2[0], idd, tiles)
    ffn_stage(L, 0, wfi[0], wfo[0], [(xm0, t0, x0, 2 + t0, T, 0) for (t0, T) in lat] + [(cm0, 0, c0, 2, CTX, 1)])
    lt = tok_tiles(SF, LRU_TO)
    tiles = [(x0, t0, To, 0, zc if i == 0 else None, zc if i == len(lt) - 1 else None, t0, i)
             for i, (t0, To) in enumerate(lt)]
    tiles.append((c0, 0, CTX, 1, zc, zc, SF, NTL))
    lruA_stage(L, 1, win, wrg, wig, tiles, lr, ea, NTT)
    tiles = [(To, 0, t0, i, x0, 2 + t0, xm1, t0) for i, (t0, To) in enumerate(lt[:NB1])]
    tiles.append((CTX, 1, SF, NTL, c0, 2, cm1, 0))
    lruB_stage(L, 1, wout, tiles, lr, ea, None, NTL, NTT, None, None)
    l1 = tok_tiles(R1, 512)
    ffn_stage(L, 1, wfi[1], wfo[1], [(xm1, t0, x1, t0, T, 0) for (t0, T) in l1] + [(cm1, 0, c1, 0, CTX, 1)])
    qkv_stage(L, 2, wqkv, [(x1, t0, T, 0, t0) for (t0, T) in l1] + [(c1, 0, CTX, 1, R1)], qd, kd, vtok, bvrow)
    na_stage(L, 2, wo, qd, kd, vtok, R1, tid, tbd, idd, x1, xm2, NQR)
    ffn_stage(L, 2, wfi[2], wfo[2], [(xm2, t0, x2h, 15 + t0, T, 0) for (t0, T) in tok_tiles(R2, 512)])
    ct = tok_tiles(W, CONV_TO)
    tiles = [(x2h, t0, xm3, t0, To, 0, zc if i == 0 else None, None) for i, (t0, To) in enumerate(ct)]
    conv_stage(L, 3, 1, a_w1[1], a_w2[1], idd, tiles)
    lw = tok_tiles(W, 512)
    ffn_stage(L, 3, wfi[3], wfo[3], [(xm3, t0, x3, t0, T, 0) for (t0, T) in lw])
    final_norm_stage(L, [(x3, t0, od, t0, T) for (t0, T) in lw])
    return L


def kernel(**I):
    I = {k: np.asarray(v) for k, v in I.items()}
    cores = [(b, hf) for b in range(BATCH) for hf in range(2)]
    L = build_fused()
    ident = np.eye(128, dtype=np.float32)
    bm = np.ascontiguousarray(np.concatenate([fm(I["b_mod"][i]) for i in range(4)], axis=1))
    bvrow = np.ascontiguousarray(np.tile(np.asarray(I["c_b_qkv"][0][2 * D:], np.float32)[None, :], (128, 1)))
    vecs = {hf: _vecs_host(I, hf) for hf in range(2)}
    tabs = {hf: _na_tables(I["c_rpb"][0], hf) for hf in range(2)}
    wrg = {0: I["b_w_rg"][0], 1: np.ascontiguousarray(I["b_w_rg"][0][::-1])}
    wig = {0: I["b_w_ig"][0], 1: np.ascontiguousarray(I["b_w_ig"][0][::-1])}
    ims = []
    for (b, hf) in cores:
        xs = I["x"][b] if hf == 0 else I["x"][b][::-1]
        cs = I["ctx"][b] if hf == 0 else I["ctx"][b][::-1]
        cv = np.stack([fm(I["c"][b]), fm(I["c_ctx"])], axis=2).reshape(128, 16)
        ims.append({
            "vecs": vecs[hf], "cv": np.ascontiguousarray(cv), "bm": bm, "w_mod": I["w_mod"],
            "xh": _halo(np.ascontiguousarray(xs.T), 0, 15, 15, SF), "ch": _halo(np.ascontiguousarray(cs.T), 0, 15, 15, CTX),
            "ident": ident, "a_w1": I["a_w_pw1"], "a_w2": I["a_w_pw2"], "wfi": I["w_ffn_in"], "wfo": I["w_ffn_out"],
            "win": I["b_w_in"][0], "wrg": wrg[hf], "wig": wig[hf], "wout": I["b_w_out"][0], "wqkv": I["c_w_qkv"][0],
            "wo": I["c_w_o"][0], "bvrow": bvrow, "tabi": tabs[hf][0], "tabb": tabs[hf][1]})
    res = run_launch(L, ims)
    out = np.empty((BATCH, SEQ, D), np.float32)
    for n, (b, hf) in enumerate(cores):
        o = res[n]["o"].T
        if hf == 0:
            out[b, :W, :] = o
        else:
            out[b, W:, :] = o[::-1]
    return out


def _halo(full, t0, left, right, width):
    Dd, S = full.shape
    out = np.zeros((Dd, left + width + right), full.dtype)
    a, b = t0 - left, t0 + width + right
    aa, bb = max(a, 0), min(b, S)
    out[:, aa - a:bb - a] = full[:, aa:bb]
    return out
```
